# Optimizing a Trainium2 kernel written in Bass

```python
import math
import jax, jax.numpy as jnp
from jax import lax
import numpy as np

D_MODEL = 2048
BATCH = 4
SEQ = 2048
DEPTH = 1

HEAD_DIM = 64
N_Q_HEADS = 16
N_KV_HEADS = 2
GQA_GROUP = N_Q_HEADS // N_KV_HEADS
ATTN_WIDTH = N_Q_HEADS * HEAD_DIM
KV_WIDTH = N_KV_HEADS * HEAD_DIM
WINDOW = 128
BLOCK = 128
MIX_WIDTH = D_MODEL
CONV_WIDTH = MIX_WIDTH - ATTN_WIDTH
CONV_GROUP_SIZE = 64
N_CONV_GROUPS = CONV_WIDTH // CONV_GROUP_SIZE
CONV_KERNEL = 31
IN_WIDTH = ATTN_WIDTH + 2 * KV_WIDTH + 2 * CONV_WIDTH
D_FF = 5632
LN_EPS = 1e-5
DEEPNORM_ALPHA = (2.0 * DEPTH) ** 0.25
DEEPNORM_BETA = (8.0 * DEPTH) ** -0.25
ATTN_SCALE = 1.0 / math.sqrt(HEAD_DIM)

kernel_name = "hymba_swa_sink_conformer_conv_macaron_deepnorm"


def layer_norm(x, g, b):
    xf = x.astype(jnp.float32)
    mu = jnp.mean(xf, axis=-1, keepdims=True)
    var = jnp.mean(jnp.square(xf - mu), axis=-1, keepdims=True)
    y = (xf - mu) * lax.rsqrt(var + LN_EPS) * g.astype(jnp.float32) + b.astype(jnp.float32)
    return y.astype(x.dtype)


def swiglu_ffn(x, w_gate, w_up, w_down):
    h = jax.nn.silu(x @ w_gate) * (x @ w_up)
    return h @ w_down


def sliding_window_attention_with_sinks(q, k, v, sinks):
    B, S = q.shape[0], q.shape[1]
    nb = S // BLOCK
    qb = q.reshape(B, nb, BLOCK, N_KV_HEADS, GQA_GROUP, HEAD_DIM)
    pad = ((0, 0), (BLOCK, 0), (0, 0), (0, 0))
    kp = jnp.pad(k, pad).reshape(B, nb + 1, BLOCK, N_KV_HEADS, HEAD_DIM)
    vp = jnp.pad(v, pad).reshape(B, nb + 1, BLOCK, N_KV_HEADS, HEAD_DIM)
    kb = jnp.concatenate([kp[:, :-1], kp[:, 1:]], axis=2)
    vb = jnp.concatenate([vp[:, :-1], vp[:, 1:]], axis=2)
    scores = jnp.einsum('bnqhgd,bnkhd->bnhgqk', qb, kb).astype(jnp.float32) * ATTN_SCALE
    q_rel = jnp.arange(BLOCK)[:, None] + BLOCK
    k_rel = jnp.arange(2 * BLOCK)[None, :]
    delta = q_rel - k_rel
    band = (delta >= 0) & (delta < WINDOW)
    k_abs = jnp.arange(nb)[:, None] * BLOCK - BLOCK + k_rel
    valid = band[None] & (k_abs >= 0)[:, None, :]
    scores = jnp.where(valid[None, :, None, None], scores, jnp.finfo(jnp.float32).min)
    sink = sinks.astype(jnp.float32).reshape(N_KV_HEADS, GQA_GROUP)[None, None, :, :, None, None]
    sink = jnp.broadcast_to(sink, scores.shape[:-1] + (1,))
    probs = jax.nn.softmax(jnp.concatenate([scores, sink], axis=-1), axis=-1)[..., :-1]
    out = jnp.einsum('bnhgqk,bnkhd->bnqhgd', probs.astype(v.dtype), vb)
    return out.reshape(B, S, ATTN_WIDTH)


def conformer_conv_group(u, dw_w, dw_b, ln_g, ln_b):
    a, gate = jnp.split(u, 2, axis=-1)
    h = a * jax.nn.sigmoid(gate)
    h = lax.conv_general_dilated(
        h, dw_w[:, None, :], window_strides=(1,), padding=[(CONV_KERNEL - 1, 0)],
        dimension_numbers=('NWC', 'WIO', 'NWC'), feature_group_count=CONV_WIDTH) + dw_b
    h = layer_norm(h, ln_g, ln_b)
    return jax.nn.silu(h)


def setup_inputs(seed: int = 0) -> dict:
    key = jax.random.key(seed)
    ks = jax.random.split(key, 24)
    L = DEPTH
    nrm = lambda k, shape, s: jax.random.normal(k, shape, jnp.float32) * s
    gain = lambda k, n: 1.0 + nrm(k, (L, n), 0.02)
    return {
        "x": nrm(ks[0], (BATCH, SEQ, D_MODEL), 1.0),
        "ffn1_w_gate": nrm(ks[1], (L, D_MODEL, D_FF), D_MODEL ** -0.5),
        "ffn1_w_up": nrm(ks[2], (L, D_MODEL, D_FF), D_MODEL ** -0.5),
        "ffn1_w_down": nrm(ks[3], (L, D_FF, D_MODEL), D_FF ** -0.5 * DEEPNORM_BETA),
        "ln1_g": gain(ks[4], D_MODEL),
        "ln1_b": nrm(ks[5], (L, D_MODEL), 0.02),
        "w_in": nrm(ks[6], (L, D_MODEL, IN_WIDTH), D_MODEL ** -0.5),
        "b_in": nrm(ks[7], (L, IN_WIDTH), 0.02),
        "attn_sinks": nrm(ks[8], (L, N_Q_HEADS), 0.5),
        "conv_dw_w": nrm(ks[9], (L, CONV_KERNEL, CONV_WIDTH), CONV_KERNEL ** -0.5),
        "conv_dw_b": nrm(ks[10], (L, CONV_WIDTH), 0.02),
        "conv_ln_g": gain(ks[11], CONV_WIDTH),
        "conv_ln_b": nrm(ks[12], (L, CONV_WIDTH), 0.02),
        "w_out": nrm(ks[13], (L, MIX_WIDTH, D_MODEL), MIX_WIDTH ** -0.5 * DEEPNORM_BETA),
        "b_out": nrm(ks[14], (L, D_MODEL), 0.02),
        "ln2_g": gain(ks[15], D_MODEL),
        "ln2_b": nrm(ks[16], (L, D_MODEL), 0.02),
        "ffn2_w_gate": nrm(ks[17], (L, D_MODEL, D_FF), D_MODEL ** -0.5),
        "ffn2_w_up": nrm(ks[18], (L, D_MODEL, D_FF), D_MODEL ** -0.5),
        "ffn2_w_down": nrm(ks[19], (L, D_FF, D_MODEL), D_FF ** -0.5 * DEEPNORM_BETA),
        "ln3_g": gain(ks[20], D_MODEL),
        "ln3_b": nrm(ks[21], (L, D_MODEL), 0.02),
    }


def reference(x, ffn1_w_gate, ffn1_w_up, ffn1_w_down, ln1_g, ln1_b, w_in, b_in, attn_sinks,
              conv_dw_w, conv_dw_b, conv_ln_g, conv_ln_b, w_out, b_out, ln2_g, ln2_b,
              ffn2_w_gate, ffn2_w_up, ffn2_w_down, ln3_g, ln3_b):
    B, S = x.shape[0], x.shape[1]
    split_points = [ATTN_WIDTH, ATTN_WIDTH + KV_WIDTH, ATTN_WIDTH + 2 * KV_WIDTH]
    for l in range(DEPTH):
        x = layer_norm(DEEPNORM_ALPHA * x + 0.5 * swiglu_ffn(x, ffn1_w_gate[l], ffn1_w_up[l], ffn1_w_down[l]),
                       ln1_g[l], ln1_b[l])
        u = x @ w_in[l] + b_in[l]
        q, k, v, conv_in = jnp.split(u, split_points, axis=-1)
        q = q.reshape(B, S, N_Q_HEADS, HEAD_DIM)
        k = k.reshape(B, S, N_KV_HEADS, HEAD_DIM)
        v = v.reshape(B, S, N_KV_HEADS, HEAD_DIM)
        attn_out = sliding_window_attention_with_sinks(q, k, v, attn_sinks[l])
        conv_out = conformer_conv_group(conv_in, conv_dw_w[l], conv_dw_b[l],
                                        conv_ln_g[l], conv_ln_b[l])
        mixed = jnp.concatenate([attn_out, conv_out], axis=-1) @ w_out[l] + b_out[l]
        x = layer_norm(DEEPNORM_ALPHA * x + mixed, ln2_g[l], ln2_b[l])
        x = layer_norm(DEEPNORM_ALPHA * x + 0.5 * swiglu_ffn(x, ffn2_w_gate[l], ffn2_w_up[l], ffn2_w_down[l]),
                       ln3_g[l], ln3_b[l])
    return x
```

```python
import bisect
import os
from contextlib import ExitStack

import numpy as np
import concourse.bass as bass
import concourse.mybir as mybir
from concourse.bass_utils import run_bass_kernel_spmd

F32 = mybir.dt.float32
F32R = mybir.dt.float32r
BF16 = mybir.dt.bfloat16
AF = mybir.ActivationFunctionType
ALU = mybir.AluOpType
AX = mybir.AxisListType

D = 2048
DFF = 5632
NFC = DFF // 128
KC = D // 128
T = 1152
TOWN = 1024
INW = 3328
ALPHA = 2.0 ** 0.25
EPS = 1e-5
NEG = -30000.0
FB = 11
NFB = NFC // FB

C_LN1G, C_LN1B, C_LN2G, C_LN2B, C_LN3G, C_LN3B = 0, 16, 32, 48, 64, 80
C_BIN = 96
C_BOUT = 122
C_CW = 138
C_CB = 386
C_CLG = 394
C_CLB = 402
C_SINK = 410
C_BV = 426
C_FLAG = 554
C_BKD = 555
NCOL = 557
D_ALN1G, D_ALN1B, D_ALN2G, D_ALN2B, D_EPS, D_NSINK, D_NMSINK = 0, 16, 32, 48, 64, 65, 81
NCOL2 = 85

SAME_ENGINE_SYNC = True


class Tracker:
    def __init__(self, nc, stack):
        self.nc = nc
        self.stack = stack
        self.eng = {}
        for name, obj in (("pe", nc.tensor), ("act", nc.scalar), ("dve", nc.vector),
                          ("pool", nc.gpsimd), ("sp", nc.sync)):
            sem = stack.enter_context(nc.semaphore("prog_" + name))
            self.eng[name] = dict(name=name, obj=obj, sem=sem, insts=[], sig_idx=[], sig_cnt=[],
                                  waited={})
        self.last_write = {}
        self.readers = {}
        self.nsem = 0

    def new_sem(self, name):
        self.nsem += 1
        return self.stack.enter_context(self.nc.semaphore(f"{name}_{self.nsem}"))

    def _signal_count(self, E, idx):
        pos = bisect.bisect_left(E["sig_idx"], idx)
        if pos < len(E["sig_idx"]):
            return E["sig_cnt"][pos]
        last = len(E["insts"]) - 1
        E["insts"][last].then_inc(E["sem"], 1)
        cnt = len(E["sig_idx"]) + 1
        E["sig_idx"].append(last)
        E["sig_cnt"].append(cnt)
        return cnt

    def _wait(self, E, ev):
        if ev is None:
            return
        if ev[0] == "e":
            P = self.eng[ev[1]]
            if P is E:
                if E["name"] in ("pe", "sp", "pool") or not SAME_ENGINE_SYNC:
                    return
            cnt = self._signal_count(P, ev[2])
            sem = P["sem"]
        else:
            sem, cnt = ev[1], ev[2]
        key = id(sem)
        if E["waited"].get(key, 0) >= cnt:
            return
        E["obj"].wait_ge(sem, cnt)
        E["waited"][key] = cnt

    def _deps(self, reads, writes):
        deps = []
        for k in reads:
            w = self.last_write.get(k)
            if w is not None:
                deps.append(w)
        for k in writes:
            w = self.last_write.get(k)
            if w is not None:
                deps.append(w)
            rd = self.readers.get(k)
            if rd:
                for en, v in rd.items():
                    if en == "_d":
                        deps.extend(v)
                    else:
                        deps.append(("e", en, v))
        return deps

    def _record(self, ev, reads, writes):
        for k in reads:
            rd = self.readers.setdefault(k, {})
            if ev[0] == "e":
                rd[ev[1]] = ev[2]
            else:
                rd.setdefault("_d", []).append(ev)
        for k in writes:
            self.last_write[k] = ev
            self.readers[k] = {}

    def op(self, eng, fn, reads=(), writes=(), sig=False):
        E = self.eng[eng]
        ps_r = [k for k in reads if isinstance(k, tuple) and k[0] in ("psf", "psb")]
        if ps_r:
            writes = list(writes) + ps_r
        for ev in self._deps(reads, writes):
            self._wait(E, ev)
        inst = fn(E["obj"])
        idx = len(E["insts"])
        E["insts"].append(inst)
        if sig or eng in ("act", "dve", "pool"):
            self._signal_count(E, idx)
        ev = ("e", eng, idx)
        self._record(ev, reads, writes)
        return ev

    def dma(self, queue, out, in_, sem, semstate, reads=(), writes=()):
        E = self.eng[queue]
        for ev in self._deps(reads, writes):
            self._wait(E, ev)
        E["obj"].dma_start(out=out, in_=in_).then_inc(sem, 16)
        semstate["v"] = semstate.get("v", 0) + 16
        ev = ("d", sem, semstate["v"])
        self._record(ev, reads, writes)
        return ev

    def barrier(self):
        names = ["pe", "act", "dve"]
        evs = {}
        for n in names:
            E = self.eng[n]
            if E["insts"]:
                evs[n] = ("e", n, len(E["insts"]) - 1)
        for n in names + ["pool", "sp"]:
            for m, ev in evs.items():
                if m != n:
                    self._wait(self.eng[n], ev)


def tiles_of(t0, n):
    return range(t0 // 128, (t0 + n - 1) // 128 + 1)


def build_program(stage=3):
    nc = bass.Bass("TRN2", target_bir_lowering=False)
    dt_in = lambda name, shape: nc.dram_tensor(name, shape, F32, kind="ExternalInput").ap()
    xin = dt_in("xin", [T, D])
    cst_d = dt_in("cst", [128, NCOL])
    msk_d = dt_in("msk", [128, 512])
    idn_d = dt_in("idn", [128, 128])
    w1g = dt_in("w1g", [D, DFF]); w1u = dt_in("w1u", [D, DFF]); w1d = dt_in("w1d", [DFF, D])
    w2g = dt_in("w2g", [D, DFF]); w2u = dt_in("w2u", [D, DFF]); w2d = dt_in("w2d", [DFF, D])
    win = dt_in("win", [D, INW]); wout = dt_in("wout", [D, D])
    y = nc.dram_tensor("y", [TOWN, D], F32, kind="ExternalOutput").ap()

    with ExitStack() as stack:
        tr = Tracker(nc, stack)
        sb = lambda name, shape, dt, st=stack: st.enter_context(nc.sbuf_tensor(name, shape, dt))

        xb = sb("xb", [128, KC, T], BF16)
        r = sb("r", [128, KC, T], F32)
        NW = 4
        wring = sb("wring", [128, NW, KC, 128], BF16)
        cst = sb("cst_sb", [128, NCOL], F32)
        cst2 = sb("cst2", [128, NCOL2], F32)
        msk = sb("msk_sb", [128, 512], F32)
        idf = sb("idf", [128, 128], F32)
        idb = sb("idb", [128, 128], BF16)
        avgD = sb("avgD", [128, 128], F32)
        avgC = sb("avgC", [128, 128], F32)
        ln_mean = sb("ln_mean", [128, 512], F32)
        ln_rstd = sb("ln_rstd", [128, 512], F32)
        ln_nmr = sb("ln_nmr", [128, 512], F32)
        ln_sq = sb("ln_sq", [128, 2, 512], F32R)
        avgDr = sb("avgDr", [128, 128], F32R)
        avgCr = sb("avgCr", [128, 128], F32R)
        ln_t = sb("ln_t", [128, 2, 512], F32)
        psf = [stack.enter_context(nc.psum_tensor(f"psf{i}", [128, 512], F32)) for i in range(7)]
        psb = [stack.enter_context(nc.psum_tensor(f"psb{i}", [128, 1024], BF16)) for i in range(1)]
        st = dict(pf=0, pb=0, sq=0, lt=0, ring=list(range(7)))

        def psum_f():
            ring = st["ring"]
            i = ring[st["pf"] % len(ring)]
            st["pf"] += 1
            return psf[i], ("psf", i)

        def psum_b():
            return psb[0], ("psb", 0)

        csem = tr.new_sem("cld"); cstate = {}
        tr.dma("sp", cst[:], cst_d[:, :], csem, cstate, writes=["cst"])
        tr.dma("sp", msk[:], msk_d[:, :], csem, cstate, writes=["msk"])
        tr.dma("sp", idf[:], idn_d[:, :], csem, cstate, writes=["idf"])
        for k_ in ("cst", "msk", "idf"):
            tr.last_write[k_] = ("d", csem, cstate["v"])
        tr.op("dve", lambda e: e.tensor_copy(idb[:], idf[:]), reads=["idf"], writes=["idb"])
        tr.op("dve", lambda e: e.memset(avgD[:], 1.0 / D), writes=["avgD"])
        tr.op("dve", lambda e: e.memset(avgC[:], 1.0 / 1024), writes=["avgC"])
        tr.op("dve", lambda e: e.tensor_copy(avgDr[:], avgD[:]), reads=["avgD"], writes=["avgDr"])
        tr.op("dve", lambda e: e.tensor_copy(avgCr[:], avgC[:]), reads=["avgC"], writes=["avgCr"])
        tr.op("dve", lambda e: e.memset(cst2[:, D_EPS:D_EPS + 1], EPS), writes=["cst2"])
        tr.op("dve", lambda e: e.tensor_scalar(cst2[:, 0:32], cst[:, C_LN1G:C_LN1G + 32], ALPHA, None, ALU.mult),
              reads=["cst"], writes=["cst2"])
        tr.op("dve", lambda e: e.tensor_scalar(cst2[:, 32:64], cst[:, C_LN2G:C_LN2G + 32], ALPHA, None, ALU.mult),
              reads=["cst"], writes=["cst2"])
        tr.op("dve", lambda e: e.tensor_scalar(cst2[:, D_NSINK:D_NSINK + 16], cst[:, C_SINK:C_SINK + 16], -1.0, None, ALU.mult),
              reads=["cst"], writes=["cst2"])
        tr.op("dve", lambda e: e.tensor_reduce(cst2[:, D_NMSINK:D_NMSINK + 4], cst[:, C_SINK:C_SINK + 16].rearrange("p (b h) -> p b h", h=4),
                                               AX.X, ALU.max), reads=["cst"], writes=["cst2"])
        tr.op("dve", lambda e: e.tensor_scalar(cst2[:, D_NMSINK:D_NMSINK + 4], cst2[:, D_NMSINK:D_NMSINK + 4], -1.0, None, ALU.mult),
              reads=["cst2"], writes=["cst2"])

        class WStream:
            def __init__(self, items, nslots, keyname, dst_fn, depth=None):
                self.items = items
                self.n = nslots
                self.key = keyname
                self.dst_fn = dst_fn
                self.issued = 0
                self.cur = 0
                self.sems = [tr.new_sem(keyname) for _ in range(nslots)]
                self.state = [dict() for _ in range(nslots)]
                self.depth = depth or nslots

            def _issue(self, i):
                s = i % self.n
                for src, sel in self.items[i]:
                    tr.dma("pool", self.dst_fn(s, sel), src, self.sems[s], self.state[s],
                           writes=[(self.key, s)])

            def prefetch(self, upto=None):
                while self.issued < min(len(self.items), self.cur + (upto or self.n)):
                    self._issue(self.issued)
                    self.issued += 1

            def get(self):
                i = self.cur
                while self.issued <= i:
                    self._issue(self.issued)
                    self.issued += 1
                self.cur += 1
                return i % self.n

        def colchunk(W, j):
            return [(W[:, j * 128:(j + 1) * 128].rearrange("(k p) f -> p k f", p=128), None)]

        def kdup(hk):
            src = win[:, 1024 + hk * 64:1024 + (hk + 1) * 64].rearrange("(k p) f -> p k f", p=128)
            return [(src, 0), (src, 1)]

        witems = []
        for j in range(NFC):
            witems.append(colchunk(w1g, j)); witems.append(colchunk(w1u, j))
        witems.append(kdup(0)); witems.append(kdup(1))
        witems.append(colchunk(win, 9))
        for j in range(8):
            witems.append(colchunk(win, j))
        for cc in range(8):
            witems.append(colchunk(win, 10 + cc)); witems.append(colchunk(win, 18 + cc))
        for half in range(2):
            for kk in range(KC):
                witems.append([(wout[half * 1024:(half + 1) * 1024, kk * 128:(kk + 1) * 128].rearrange("(k p) f -> p k f", p=128), "h8")])
        for j in range(NFC):
            witems.append(colchunk(w2g, j)); witems.append(colchunk(w2u, j))

        def wdst(s, sel):
            if sel is None:
                return wring[:, s]
            if sel == "h8":
                return wring[:, s, 0:8]
            return wring[:, s, :, sel * 64:(sel + 1) * 64]

        ws = WStream(witems, NW, "w", wdst)

        def xkeys(name, k, t0, n):
            return [(name, k, t) for t in tiles_of(t0, n)]

        def proj_group(slot, t0, n, out_ps, out_key, src_fn=None, nk=KC):
            if src_fn is None:
                src_fn = lambda k, t0, n: (xb[:, k, t0:t0 + n], xkeys("xb", k, t0, n))
            for k in range(nk):
                ap, keys = src_fn(k, t0, n)
                tr.op("pe", lambda e: e.matmul(out_ps[:, 0:n], wring[:, slot, k, :], ap,
                                               start=(k == 0), stop=(k == nk - 1)),
                      reads=[("w", slot)] + keys, writes=[out_key], sig=(k == nk - 1))

        def layernorm(nch, src_fn, src_keys_fn, groups, avg, emit_out, hook=None, after_group=None, stat_banks=None):
            avgr = avgDr if avg is avgD else avgCr

            sbi = [0]

            def stat_bank():
                if stat_banks is None:
                    return psum_f()
                i_ = stat_banks[sbi[0] % len(stat_banks)]; sbi[0] += 1
                return psf[i_], ("psf", i_)

            def stats(t0, n):
                p1, k1 = stat_bank()
                p2, k2 = stat_bank()
                for k in range(nch):
                    sqi = st["sq"] % 2; st["sq"] += 1
                    sk = src_keys_fn(k, t0, n)
                    tr.op("act", lambda e: e.activation(ln_sq[:, sqi, 0:n], src_fn(k, t0, n), AF.Square),
                          reads=sk, writes=[("lnsq", sqi)])
                    tr.op("pe", lambda e: e.matmul(p1[:, 0:n], avg[:], src_fn(k, t0, n),
                                                   start=(k == 0), stop=(k == nch - 1)),
                          reads=sk + ["avg"], writes=[k1])
                    tr.op("pe", lambda e: e.matmul(p2[:, 0:n], avgr[:], ln_sq[:, sqi, 0:n],
                                                   start=(k == 0), stop=(k == nch - 1)),
                          reads=[("lnsq", sqi), "avg"], writes=[k2], sig=True)
                return p1, k1, p2, k2

            def finalize(t0, n, p1, k1, p2, k2):
                tr.op("dve", lambda e: e.tensor_copy(ln_mean[:, 0:n], p1[:, 0:n]), reads=[k1], writes=["lnmean"])
                tr.op("dve", lambda e: e.tensor_tensor(ln_rstd[:, 0:n], ln_mean[:, 0:n], ln_mean[:, 0:n], ALU.mult),
                      reads=["lnmean"], writes=["lnrstd"])
                tr.op("dve", lambda e: e.tensor_tensor(ln_rstd[:, 0:n], p2[:, 0:n], ln_rstd[:, 0:n], ALU.subtract),
                      reads=[k2, "lnrstd"], writes=["lnrstd"])
                tr.op("act", lambda e: e.activation(ln_rstd[:, 0:n], ln_rstd[:, 0:n], AF.Sqrt,
                                                    bias=cst2[:, D_EPS:D_EPS + 1], scale=1.0),
                      reads=["lnrstd", "cst2"], writes=["lnrstd"])
                tr.op("dve", lambda e: e.reciprocal(ln_rstd[:, 0:n], ln_rstd[:, 0:n]), reads=["lnrstd"], writes=["lnrstd"])
                tr.op("dve", lambda e: e.scalar_tensor_tensor(ln_nmr[:, 0:n], ln_mean[:, 0:n], -1.0, ln_rstd[:, 0:n],
                                                              ALU.mult, ALU.mult),
                      reads=["lnmean", "lnrstd"], writes=["lnnmr"])

            def normalize(t0, n):
                for k in range(nch):
                    ti = st["lt"] % 2; st["lt"] += 1
                    sk = src_keys_fn(k, t0, n)
                    tr.op("dve", lambda e: e.tensor_tensor(ln_t[:, ti, 0:n], src_fn(k, t0, n), ln_rstd[:, 0:n], ALU.mult),
                          reads=sk + ["lnrstd"], writes=[("lnt", ti)])
                    tr.op("dve", lambda e: e.tensor_tensor(ln_t[:, ti, 0:n], ln_t[:, ti, 0:n], ln_nmr[:, 0:n], ALU.add),
                          reads=[("lnt", ti), "lnnmr"], writes=[("lnt", ti)])
                    emit_out(k, t0, n, ln_t[:, ti, 0:n], ("lnt", ti))
                    if hook is not None:
                        hook()

            pend = stats(*groups[0])
            for gi, (t0, n) in enumerate(groups):
                nxt = stats(*groups[gi + 1]) if gi + 1 < len(groups) else None
                finalize(t0, n, *pend)
                normalize(t0, n)
                if after_group is not None:
                    after_group(t0, n)
                pend = nxt

        def rsrc(k, t0, n):
            return r[:, k, t0:t0 + n]

        def rkeys(k, t0, n):
            return xkeys("r", k, t0, n)

        def ln_main_out(gcol, bcol, agcol, abcol, final=False):
            def emit(k, t0, n, tap, tkey):
                if not final:
                    tr.op("act", lambda e: e.activation(xb[:, k, t0:t0 + n], tap, AF.Identity,
                                                        bias=cst[:, bcol + k:bcol + k + 1], scale=cst[:, gcol + k:gcol + k + 1]),
                          reads=[tkey, "cst"], writes=xkeys("xb", k, t0, n))
                    tr.op("act", lambda e: e.activation(r[:, k, t0:t0 + n], tap, AF.Identity,
                                                        bias=cst2[:, abcol + k:abcol + k + 1], scale=cst2[:, agcol + k:agcol + k + 1]),
                          reads=[tkey, "cst2"], writes=xkeys("r", k, t0, n))
                elif k % 2 == 0:
                    tr.op("act", lambda e: e.activation(r[:, k, t0:t0 + n], tap, AF.Identity,
                                                        bias=cst[:, bcol + k:bcol + k + 1], scale=cst[:, gcol + k:gcol + k + 1]),
                          reads=[tkey, "cst"], writes=xkeys("r", k, t0, n))
                else:
                    tr.op("act", lambda e: e.activation(r[:, k, t0:t0 + n], tap, AF.Identity,
                                                        bias=cst[:, bcol + k:bcol + k + 1], scale=cst[:, gcol + k:gcol + k + 1]),
                          reads=[tkey, "cst"], writes=xkeys("r", k, t0, n))
            return emit

        def ffn(Wd, groups, tagn):
            with ExitStack() as ph:
                hT = sb(f"hT{tagn}", [128, FB, T], BF16, ph)
                sil = sb(f"sil{tagn}", [128, 2, 512], F32, ph)
                dring = sb(f"dring{tagn}", [128, 2, FB, 512], BF16, ph)
                ditems = []
                for fb in range(NFB):
                    for dq in range(4):
                        src = Wd[fb * FB * 128:(fb + 1) * FB * 128, dq * 512:(dq + 1) * 512].rearrange("(c p) d -> p c d", p=128)
                        ditems.append([(src, None)])
                ds = WStream(ditems, 2, f"d{tagn}", lambda s, sel: dring[:, s])
                si = 0
                for fb in range(NFB):
                    for c in range(FB):
                        if c == 3:
                            ds.prefetch()
                        sg = ws.get(); su = ws.get()
                        for (t0, n) in groups:
                            pg, kg = psum_f(); pu, ku = psum_f()
                            proj_group(sg, t0, n, pg, kg)
                            proj_group(su, t0, n, pu, ku)
                            s_i = si % 2; si += 1
                            tr.op("act", lambda e, s_i=s_i, pg=pg: e.activation(sil[:, s_i, 0:n], pg[:, 0:n], AF.Silu),
                                  reads=[kg], writes=[("sil", s_i)])
                            tr.op("dve", lambda e, s_i=s_i, pu=pu, c=c: e.tensor_tensor(hT[:, c, t0:t0 + n], sil[:, s_i, 0:n], pu[:, 0:n], ALU.mult),
                                  reads=[("sil", s_i), ku], writes=xkeys("h", c, t0, n))
                        ws.prefetch()
                    for dq in range(4):
                        sd = ds.get()
                        for dk in range(4):
                            kk = dq * 4 + dk
                            for (t0, n) in groups:
                                pd, kd = psum_f()
                                for c in range(FB):
                                    tr.op("pe", lambda e, c=c, pd=pd: e.matmul(pd[:, 0:n], dring[:, sd, c, dk * 128:(dk + 1) * 128],
                                                                              hT[:, c, t0:t0 + n], start=(c == 0), stop=(c == FB - 1)),
                                          reads=[(f"d{tagn}", sd)] + xkeys("h", c, t0, n), writes=[kd], sig=(c == FB - 1))
                                tr.op("dve", lambda e, pd=pd, kk=kk: e.scalar_tensor_tensor(r[:, kk, t0:t0 + n], pd[:, 0:n], 0.5, r[:, kk, t0:t0 + n],
                                                                                            ALU.mult, ALU.add),
                                      reads=[kd] + xkeys("r", kk, t0, n), writes=xkeys("r", kk, t0, n))
                        ds.prefetch()
                tr.barrier()

        ws.prefetch(upto=2)
        with ExitStack() as ph:
            NXT = 4
            xt = sb("xt", [128, NXT, D], F32, ph)
            xsem = [tr.new_sem("xl") for _ in range(NXT)]
            xstate = [dict() for _ in range(NXT)]
            for i in range(int(os.environ.get('K_NT0', T // 128))):
                s = i % NXT
                tr.dma("sp", xt[:, s], xin[i * 128:(i + 1) * 128, :], xsem[s], xstate[s], writes=[("xt", s)])
                for kq in range(4):
                    pt, kt = psum_f()
                    for c in range(4):
                        k = kq * 4 + c
                        tr.op("pe", lambda e, k=k, c=c, pt=pt: e.transpose(pt[:, c * 128:(c + 1) * 128], xt[:, s, k * 128:(k + 1) * 128], idf[:]),
                              reads=[("xt", s), "idf"], writes=[kt], sig=(c == 3))
                    pv = pt[:, :].rearrange("p (c t) -> p c t", c=4)
                    tr.op("act", lambda e, pv=pv, kq=kq: e.activation(r[:, kq * 4:(kq + 1) * 4, i * 128:(i + 1) * 128], pv, AF.Identity, scale=ALPHA),
                          reads=[kt], writes=[("r", kq * 4 + c, i) for c in range(4)])
                    tr.op("dve", lambda e, pv=pv, kq=kq: e.tensor_copy(xb[:, kq * 4:(kq + 1) * 4, i * 128:(i + 1) * 128], pv),
                          reads=[kt], writes=[("xb", kq * 4 + c, i) for c in range(4)])
            tr.barrier()
        ws.prefetch()

        G3 = [(0, 384), (384, 384), (768, 384)]
        G2 = [(128, 512), (640, 512)]

        if stage >= 1:
            if not os.environ.get("K_SKIP_FFN"):
                ffn(w1d, G3, 1)
            if not os.environ.get("K_NO_LN"):
                layernorm(KC, rsrc, rkeys, G3, avgD, ln_main_out(C_LN1G, C_LN1B, D_ALN1G, D_ALN1B))

        def mixer():
            with ExitStack() as m1:
                qT = sb("qT", [128, 8, TOWN], BF16, m1)
                kd_ = sb("kdup", [128, 2, T], BF16, m1)
                vv = sb("vv", [128, 9, 128], BF16, m1)
                hall = sb("hall", [128, 8, T], BF16, m1)
                dg = sb("dgring", [128, 16, 128], BF16, m1)
                sg_t = sb("sgt", [128, 2, 384], F32, m1)
                stt = sb("stt", [128, 4, 32], F32, m1)
                acc = xb[:].rearrange("p k t -> p (k t)").bitcast(F32)[:, 0:8 * TOWN].rearrange("p (c t) -> p c t", c=8)
                pT1 = sb("pT1", [128, 8, 128], BF16, m1)
                ao2 = sb("ao", [128, 2, 1024], BF16, m1)
                s0 = sb("s_buf0", [128, 4, 256], F32, m1)
                s_buf = [s0[:], ln_t[:].rearrange("p a (h k) -> p (a h) k", h=2)]
                s_key = [["s_buf0"], [("lnt", 0), ("lnt", 1)]]
                e_buf = [ln_mean[:].bitcast(BF16).rearrange("p (h k) -> p h k", h=4),
                         ln_rstd[:].bitcast(BF16).rearrange("p (h k) -> p h k", h=4)]
                e_key = ["lnmean", "lnrstd"]
                pT_buf = [ln_nmr[:].bitcast(BF16).rearrange("p (a b) -> p a b", a=8), pT1[:]]
                pT_key = ["lnnmr", "pT1"]

                for hk in range(2):
                    s_ = ws.get()
                    for (t0, n) in G3:
                        p, kp = psum_f()
                        proj_group(s_, t0, n, p, kp)
                        tr.op("act", lambda e: e.activation(kd_[:, hk, t0:t0 + n], p[:, 0:n], AF.Identity,
                                                            bias=cst[:, C_BKD + hk:C_BKD + hk + 1], scale=1.0),
                              reads=[kp, "cst"], writes=[("kd", hk, t) for t in tiles_of(t0, n)])
                    ws.prefetch()
                s_ = ws.get()
                for i in range(9):
                    p, kp = psum_f()
                    for k in range(KC):
                        tr.op("pe", lambda e: e.matmul(p[:, 0:128], xb[:, k, i * 128:(i + 1) * 128], wring[:, s_, k, :],
                                                       start=(k == 0), stop=(k == KC - 1)),
                              reads=[("w", s_), ("xb", k, i)], writes=[kp], sig=(k == KC - 1))
                    tr.op("dve", lambda e: e.tensor_tensor(vv[:, i, :], p[:, 0:128], cst[:, C_BV:C_BV + 128], ALU.add),
                          reads=[kp, "cst"], writes=[("v", i)])
                ws.prefetch()
                for j in range(8):
                    s_ = ws.get()
                    for (t0, n) in G2:
                        p, kp = psum_f()
                        proj_group(s_, t0, n, p, kp)
                        tr.op("act", lambda e: e.activation(qT[:, j, t0 - 128:t0 - 128 + n], p[:, 0:n], AF.Identity,
                                                            bias=cst[:, C_BIN + j:C_BIN + j + 1], scale=1.0),
                              reads=[kp, "cst"], writes=[("q", j, t) for t in tiles_of(t0, n)])
                    ws.prefetch()
                sgi = 0
                GC = [(98, 286), (384, 384), (768, 384)]
                for cc in range(8):
                    sa = ws.get(); sgt = ws.get()
                    for (t0, n) in GC:
                        pa, ka = psum_f(); pg, kg = psum_f()
                        proj_group(sa, t0, n, pa, ka)
                        proj_group(sgt, t0, n, pg, kg)
                        s_i = sgi % 2; sgi += 1
                        tr.op("act", lambda e: e.activation(sg_t[:, s_i, 0:n], pg[:, 0:n], AF.Sigmoid,
                                                            bias=cst[:, C_BIN + 18 + cc:C_BIN + 19 + cc], scale=1.0),
                              reads=[kg, "cst"], writes=[("sg", s_i)])
                        tr.op("dve", lambda e: e.scalar_tensor_tensor(hall[:, cc, t0:t0 + n], pa[:, 0:n], cst[:, C_BIN + 10 + cc:C_BIN + 11 + cc],
                                                                      sg_t[:, s_i, 0:n], ALU.add, ALU.mult),
                              reads=[ka, ("sg", s_i), "cst"], writes=[("hglu", cc)])
                    tr.op("dve", lambda e: e.tensor_scalar(hall[:, cc, 98:128], hall[:, cc, 98:128], cst[:, C_FLAG:C_FLAG + 1], None, ALU.mult),
                          reads=[("hglu", cc), "cst"], writes=[("hglu", cc)])
                    ws.prefetch()

                SA, SB, PO = 4, 5, 6
                st["ring"] = [0, 1, 2, 3]
                NB = 32

                def tb(t):
                    return 1 + t // 4, t % 4

                def S1(t):
                    i, b = tb(t)
                    hk = b // 2
                    mcol = 0 if i == 1 else 256
                    for hq in range(4):
                        h = 4 * b + hq
                        j, half = h // 2, h % 2
                        bank = SA if hq % 2 == 0 else SB
                        tr.op("pe", lambda e: e.matmul(
                            psf[bank][:, (hq // 2) * 256:(hq // 2) * 256 + 256],
                            qT[half * 64:(half + 1) * 64, j, (i - 1) * 128:i * 128],
                            kd_[half * 64:(half + 1) * 64, hk, (i - 1) * 128:(i + 1) * 128], start=True, stop=True),
                            reads=[("q", j, i), ("kd", hk, i - 1), ("kd", hk, i)], writes=[("psf", bank)], sig=(hq >= 2))
                    for x, bank in enumerate((SA, SB)):
                        tr.op("dve", lambda e: e.scalar_tensor_tensor(
                            s_buf[t % 2][:, 2 * x:2 * x + 2, :], psf[bank][:, :].rearrange("p (h k) -> p h k", h=2), 0.125,
                            msk[:, mcol:mcol + 256].unsqueeze(1).to_broadcast([128, 2, 256]), ALU.mult, ALU.add),
                            reads=[("psf", bank), "msk"], writes=s_key[t % 2])

                def S2(t):
                    i, b = tb(t)
                    sv = stt[:, t % 4, :]
                    sk = ("stt", t % 4)
                    sb_ = s_buf[t % 2]
                    tr.op("dve", lambda e: e.tensor_reduce(sv[:, 0:4], sb_, AX.X, ALU.max), reads=s_key[t % 2], writes=[sk])
                    tr.op("dve", lambda e: e.scalar_tensor_tensor(sv[:, 4:8], sv[:, 0:4], -1.0, cst2[:, D_NSINK + 4 * b:D_NSINK + 4 * b + 4],
                                                                  ALU.mult, ALU.min),
                          reads=[sk, "cst2"], writes=[sk])
                    tr.op("dve", lambda e: e.tensor_tensor(sv[:, 12:16], cst[:, C_SINK + 4 * b:C_SINK + 4 * b + 4], sv[:, 4:8], ALU.add),
                          reads=[sk, "cst"], writes=[sk])
                    tr.op("dve", lambda e: e.memset(sv[:, 8:12], 0.0), writes=[sk])
                    for p_ in range(4):
                        tr.op("act", lambda e: e.activation(e_buf[t % 2][:, p_, :], sb_[:, p_, :], AF.Exp,
                                                            bias=sv[:, 4 + p_:5 + p_], scale=1.0, accum_out=sv[:, 8 + p_:9 + p_]),
                              reads=s_key[t % 2] + [sk], writes=[e_key[t % 2], sk])
                    tr.op("act", lambda e: e.activation(sv[:, 16:20], sv[:, 12:16], AF.Exp), reads=[sk], writes=[sk])

                def S3(t):
                    pb, kb = psum_b()
                    for p_ in range(4):
                        for blk in range(2):
                            tr.op("pe", lambda e: e.transpose(pb[:, (p_ * 2 + blk) * 128:(p_ * 2 + blk + 1) * 128],
                                                              e_buf[t % 2][:, p_, blk * 128:(blk + 1) * 128], idb[:]),
                                  reads=[e_key[t % 2], "idb"], writes=[kb], sig=(p_ == 3 and blk == 1))
                    tr.op("act", lambda e: e.activation(pT_buf[t % 2].rearrange("p a b -> p (a b)"), pb[:, :], AF.Copy),
                          reads=[kb], writes=[pT_key[t % 2]])

                def S4(t):
                    i, b = tb(t)
                    hk = b // 2
                    sv = stt[:, t % 4, :]
                    sk = ("stt", t % 4)
                    po = psf[PO]
                    ao = ao2[:, i % 2, :]
                    aok = ("ao", i % 2)
                    for p_ in range(4):
                        for blk in range(2):
                            tr.op("pe", lambda e: e.matmul(po[:, p_ * 64:(p_ + 1) * 64], pT_buf[t % 2][:, p_ * 2 + blk, :],
                                                           vv[:, i - 1 + blk, hk * 64:(hk + 1) * 64], start=(blk == 0), stop=(blk == 1)),
                                  reads=[pT_key[t % 2], ("v", i - 1 + blk)], writes=[("psf", PO)], sig=(p_ == 3 and blk == 1))
                    tr.op("dve", lambda e: e.tensor_tensor(sv[:, 16:20], sv[:, 16:20], sv[:, 8:12], ALU.add), reads=[sk], writes=[sk])
                    tr.op("dve", lambda e: e.reciprocal(sv[:, 16:20], sv[:, 16:20]), reads=[sk], writes=[sk])
                    tr.op("dve", lambda e: e.tensor_tensor(
                        ao[:, b * 256:(b + 1) * 256].rearrange("p (a two d) -> p two a d", two=2, d=64),
                        po[:, 0:256].rearrange("p (two a d) -> p two a d", two=2, a=2),
                        sv[:, 16:20].rearrange("p (two a) -> p two a", two=2).unsqueeze(3).to_broadcast([128, 2, 2, 64]), ALU.mult),
                        reads=[("psf", PO), sk], writes=[aok])
                    if b == 3:
                        def tail():
                            pb, kb = psum_b()
                            for c in range(8):
                                tr.op("pe", lambda e: e.transpose(pb[:, c * 128:(c + 1) * 128], ao[:, c * 128:(c + 1) * 128], idb[:]),
                                      reads=[aok, "idb"], writes=[kb], sig=(c == 7))
                            tr.op("act", lambda e: e.activation(qT[:, 0:8, (i - 1) * 128:i * 128],
                                                                pb[:, :].rearrange("p (c t) -> p c t", c=8), AF.Copy),
                                  reads=[kb], writes=[("q", c, i) for c in range(8)])
                        deferred.append(tail)

                deferred = []

                def attn_iter(it):
                    for stage_fn, t in ((S2, it + 2), (S1, it + 3), (S3, it + 1), (S4, it)):
                        if 0 <= t < NB:
                            stage_fn(t)

                SEG = 7
                NDG = 16
                taps = [(cc, jt) for cc in range(8) for jt in range(31)]
                segs = [taps[i_:i_ + SEG] for i_ in range(0, len(taps), SEG)]
                dslot = {}

                def gen_diags(seg):
                    for (cc, jt) in seg:
                        di = len(dslot) % NDG
                        dslot[(cc, jt)] = di
                        if len(dslot) % SEG in (2, 4, 6):
                            tr.op("act", lambda e: e.activation(dg[:, di, :], idb[:], AF.Identity,
                                                                scale=cst[:, C_CW + jt * 8 + cc:C_CW + jt * 8 + cc + 1]),
                                  reads=["idb", "cst"], writes=[("dg", di)])
                        else:
                            tr.op("dve", lambda e: e.tensor_scalar(dg[:, di, :], idb[:], cst[:, C_CW + jt * 8 + cc:C_CW + jt * 8 + cc + 1], None, ALU.mult),
                                  reads=["idb", "cst"], writes=[("dg", di)])

                it = -3
                pcs = None
                gen_diags(segs[0])
                for si, seg in enumerate(segs):
                    if si + 1 < len(segs):
                        gen_diags(segs[si + 1])
                    if it < NB:
                        attn_iter(it); it += 1
                    for (cc, jt) in seg:
                        if jt == 0:
                            pcs = [psum_f(), psum_f()]
                        di = dslot[(cc, jt)]
                        for gi in range(2):
                            pc, kc = pcs[gi]
                            o0 = 98 + jt + gi * 512
                            tr.op("pe", lambda e: e.matmul(pc[:, 0:512], dg[:, di, :], hall[:, cc, o0:o0 + 512],
                                                           start=(jt == 0), stop=(jt == 30)),
                                  reads=[("dg", di), ("hglu", cc)], writes=[kc], sig=(jt == 30))
                        if jt == 30:
                            for gi in range(2):
                                pc, kc = pcs[gi]
                                tr.op("act", lambda e: e.activation(acc[:, cc, gi * 512:(gi + 1) * 512], pc[:, 0:512], AF.Identity,
                                                                    bias=cst[:, C_CB + cc:C_CB + cc + 1], scale=1.0),
                                      reads=[kc, "cst"], writes=[("acc", cc)])
                    while deferred:
                        deferred.pop(0)()
                while it < NB:
                    attn_iter(it); it += 1
                    while deferred:
                        deferred.pop(0)()
                st["ring"] = list(range(7))

                def outproj_unit(kk, half, slot, t0, n):
                    p, kp = psum_f()
                    for k in range(8):
                        if half == 0:
                            ap, keys = qT[:, k, t0 - 128:t0 - 128 + n], [("q", k, t) for t in tiles_of(t0, n)]
                        else:
                            ap, keys = hall[:, k, t0 - 128:t0 - 128 + n], [("hglu", k)]
                        tr.op("pe", lambda e: e.matmul(p[:, 0:n], wring[:, slot, k, :], ap, start=(k == 0), stop=(k == 7)),
                              reads=[("w", slot)] + keys, writes=[kp], sig=(k == 7))
                    if half == 0:
                        tr.op("dve", lambda e: e.tensor_tensor(r[:, kk, t0:t0 + n], p[:, 0:n], r[:, kk, t0:t0 + n], ALU.add),
                              reads=[kp] + xkeys("r", kk, t0, n), writes=xkeys("r", kk, t0, n))
                    else:
                        tr.op("dve", lambda e: e.scalar_tensor_tensor(r[:, kk, t0:t0 + n], p[:, 0:n], cst[:, C_BOUT + kk:C_BOUT + kk + 1],
                                                                      r[:, kk, t0:t0 + n], ALU.add, ALU.add),
                              reads=[kp, "cst"] + xkeys("r", kk, t0, n), writes=xkeys("r", kk, t0, n))

                def outproj_chunk(kk, half):
                    s_ = ws.get()
                    for (t0, n) in G2:
                        outproj_unit(kk, half, s_, t0, n)
                    ws.prefetch()

                pass1 = list(range(KC))

                def hook():
                    if pass1:
                        outproj_chunk(pass1.pop(0), 0)

                def conv_out(k, t0, n, tap, tkey):
                    tr.op("act", lambda e: e.activation(hall[:, k, t0:t0 + n], tap, AF.Silu,
                                                        bias=cst[:, C_CLB + k:C_CLB + k + 1], scale=cst[:, C_CLG + k:C_CLG + k + 1]),
                          reads=[tkey, "cst"], writes=[("hglu", k)])
                st["ring"] = [0, 1, 2]
                layernorm(8, lambda k, t0, n: acc[:, k, t0:t0 + n], lambda k, t0, n: [("acc", k)],
                          [(0, 512), (512, 512)], avgC, conv_out, hook=hook, stat_banks=[3, 4, 5, 6])
                st["ring"] = list(range(7))
                while pass1:
                    hook()
                for kk in range(KC):
                    outproj_chunk(kk, 1)
                tr.barrier()
            layernorm(KC, rsrc, rkeys, G2, avgD, ln_main_out(C_LN2G, C_LN2B, D_ALN2G, D_ALN2B))

        if stage >= 2:
            mixer()

        ffn(w2d, G2, 2)
        with ExitStack() as ph:
            ot = sb("ot", [128, 2, D], F32, ph)
            osem = [tr.new_sem("ost") for _ in range(2)]
            ostate = [dict(), dict()]

            def emit_output(t0, n):
                for i in tiles_of(t0, n):
                    s = i % 2
                    for kq in range(4):
                        pt, kt = psum_f()
                        for c in range(4):
                            k = kq * 4 + c
                            tr.op("pe", lambda e: e.transpose(pt[:, c * 128:(c + 1) * 128], r[:, k, i * 128:(i + 1) * 128], idf[:]),
                                  reads=[("r", k, i), "idf"], writes=[kt], sig=(c == 3))
                        tr.op("act", lambda e: e.activation(ot[:, s, kq * 512:(kq + 1) * 512], pt[:, :], AF.Copy),
                              reads=[kt], writes=[("ot", s, kq)])
                    tr.dma("sp", y[(i - 1) * 128:i * 128, :], ot[:, s], osem[s], ostate[s],
                           reads=[("ot", s, kq) for kq in range(4)])

            st["ring"] = [0, 1, 2]
            layernorm(KC, rsrc, rkeys, G2, avgD, ln_main_out(C_LN3G, C_LN3B, 0, 0, final=True), after_group=emit_output,
                      stat_banks=[3, 4, 5, 6])
            for s in range(2):
                nc.sync.wait_ge(osem[s], ostate[s]["v"])
    return nc


_CACHE = {}


def _host_consts(inp, half):
    c = np.zeros((128, NCOL), np.float32)
    fm = lambda v: np.asarray(v, np.float32).reshape(-1, 128).T
    c[:, C_LN1G:C_LN1G + 16] = fm(inp["ln1_g"][0]); c[:, C_LN1B:C_LN1B + 16] = fm(inp["ln1_b"][0])
    c[:, C_LN2G:C_LN2G + 16] = fm(inp["ln2_g"][0]); c[:, C_LN2B:C_LN2B + 16] = fm(inp["ln2_b"][0])
    c[:, C_LN3G:C_LN3G + 16] = fm(inp["ln3_g"][0]); c[:, C_LN3B:C_LN3B + 16] = fm(inp["ln3_b"][0])
    c[:, C_BIN:C_BIN + 26] = fm(inp["b_in"][0])
    c[:, C_BOUT:C_BOUT + 16] = fm(inp["b_out"][0])
    cw = np.asarray(inp["conv_dw_w"][0], np.float32)
    for j in range(31):
        c[:, C_CW + j * 8:C_CW + j * 8 + 8] = fm(cw[j])
    c[:, C_CB:C_CB + 8] = fm(inp["conv_dw_b"][0])
    c[:, C_CLG:C_CLG + 8] = fm(inp["conv_ln_g"][0]); c[:, C_CLB:C_CLB + 8] = fm(inp["conv_ln_b"][0])
    sk = np.asarray(inp["attn_sinks"][0], np.float32).reshape(4, 4)[:, [0, 2, 1, 3]].reshape(16)
    c[:, C_SINK:C_SINK + 16] = sk[None, :]
    bin_ = np.asarray(inp["b_in"][0], np.float32)
    c[:, C_BV:C_BV + 128] = bin_[1152:1280][None, :]
    c[:, C_FLAG] = float(half)
    for hk in range(2):
        bk = bin_[1024 + hk * 64:1024 + (hk + 1) * 64]
        c[:, C_BKD + hk] = np.concatenate([bk, bk])
    return c


def _masks(half):
    qi = np.arange(128)[:, None]
    kj = np.arange(256)[None, :]
    valid = (kj >= qi + 1) & (kj <= qi + 128)
    mb = np.where(valid, 0.0, NEG).astype(np.float32)
    ma = mb.copy()
    if half == 0:
        ma[:, :128] = NEG
    return np.concatenate([ma, mb], axis=1)


def make_in_maps(inp):
    x = np.ascontiguousarray(inp["x"], np.float32)
    shared = {
        "w1g": np.ascontiguousarray(inp["ffn1_w_gate"][0], np.float32),
        "w1u": np.ascontiguousarray(inp["ffn1_w_up"][0], np.float32),
        "w1d": np.ascontiguousarray(inp["ffn1_w_down"][0], np.float32),
        "w2g": np.ascontiguousarray(inp["ffn2_w_gate"][0], np.float32),
        "w2u": np.ascontiguousarray(inp["ffn2_w_up"][0], np.float32),
        "w2d": np.ascontiguousarray(inp["ffn2_w_down"][0], np.float32),
        "win": np.ascontiguousarray(inp["w_in"][0], np.float32),
        "wout": np.ascontiguousarray(inp["w_out"][0], np.float32),
        "idn": np.eye(128, dtype=np.float32),
    }
    in_maps = []
    for c in range(8):
        b, half = c // 2, c % 2
        if half == 0:
            xi = np.concatenate([np.zeros((128, D), np.float32), x[b, 0:1024]], axis=0)
        else:
            xi = x[b, 896:2048]
        m = dict(shared)
        m["xin"] = np.ascontiguousarray(xi)
        m["cst"] = _host_consts(inp, half)
        m["msk"] = _masks(half)
        in_maps.append(m)
    return in_maps


def kernel(**inputs):
    inp = {k: np.asarray(v) for k, v in inputs.items()}
    nc = build_program()
    in_maps = make_in_maps(inp)
    res = run_bass_kernel_spmd(nc, in_maps, core_ids=list(range(8)))
    out = np.empty((4, 2048, D), np.float32)
    for c in range(8):
        b, half = c // 2, c % 2
        out[b, half * 1024:(half + 1) * 1024] = res.results[c]["y"]
    return out
```

```python
import bisect
import os
from contextlib import ExitStack

import numpy as np
import concourse.bass as bass
import concourse.mybir as mybir
from concourse.bass_utils import run_bass_kernel_spmd

F32 = mybir.dt.float32
F32R = mybir.dt.float32r
BF16 = mybir.dt.bfloat16
AF = mybir.ActivationFunctionType
ALU = mybir.AluOpType
AX = mybir.AxisListType

D = 2048
DFF = 5632
NFC = DFF // 128
KC = D // 128
T = 1152
TOWN = 1024
INW = 3328
ALPHA = 2.0 ** 0.25
EPS = 1e-5
NEG = -30000.0
FB = 11
NFB = NFC // FB

C_LN1G, C_LN1B, C_LN2G, C_LN2B, C_LN3G, C_LN3B = 0, 16, 32, 48, 64, 80
C_BIN = 96
C_BOUT = 122
C_CW = 138
C_CB = 386
C_CLG = 394
C_CLB = 402
C_SINK = 410
C_BV = 426
C_FLAG = 554
C_BKD = 555
NCOL = 557
D_ALN1G, D_ALN1B, D_ALN2G, D_ALN2B, D_EPS, D_NSINK, D_NMSINK = 0, 16, 32, 48, 64, 65, 81
NCOL2 = 85

SAME_ENGINE_SYNC = True


class Tracker:
    def __init__(self, nc, stack):
        self.nc = nc
        self.stack = stack
        self.eng = {}
        for name, obj in (("pe", nc.tensor), ("act", nc.scalar), ("dve", nc.vector),
                          ("pool", nc.gpsimd), ("sp", nc.sync)):
            sem = stack.enter_context(nc.semaphore("prog_" + name))
            self.eng[name] = dict(name=name, obj=obj, sem=sem, insts=[], sig_idx=[], sig_cnt=[],
                                  waited={})
        self.last_write = {}
        self.readers = {}
        self.nsem = 0

    def new_sem(self, name):
        self.nsem += 1
        return self.stack.enter_context(self.nc.semaphore(f"{name}_{self.nsem}"))

    def _signal_count(self, E, idx):
        pos = bisect.bisect_left(E["sig_idx"], idx)
        if pos < len(E["sig_idx"]):
            return E["sig_cnt"][pos]
        last = len(E["insts"]) - 1
        E["insts"][last].then_inc(E["sem"], 1)
        cnt = len(E["sig_idx"]) + 1
        E["sig_idx"].append(last)
        E["sig_cnt"].append(cnt)
        return cnt

    def _wait(self, E, ev):
        if ev is None:
            return
        if ev[0] == "e":
            P = self.eng[ev[1]]
            if P is E:
                if E["name"] in ("pe", "sp", "pool") or not SAME_ENGINE_SYNC:
                    return
            cnt = self._signal_count(P, ev[2])
            sem = P["sem"]
        else:
            sem, cnt = ev[1], ev[2]
        key = id(sem)
        if E["waited"].get(key, 0) >= cnt:
            return
        E["obj"].wait_ge(sem, cnt)
        E["waited"][key] = cnt

    def _deps(self, reads, writes):
        deps = []
        for k in reads:
            w = self.last_write.get(k)
            if w is not None:
                deps.append(w)
        for k in writes:
            w = self.last_write.get(k)
            if w is not None:
                deps.append(w)
            rd = self.readers.get(k)
            if rd:
                for en, v in rd.items():
                    if en == "_d":
                        deps.extend(v)
                    else:
                        deps.append(("e", en, v))
        return deps

    def _record(self, ev, reads, writes):
        for k in reads:
            rd = self.readers.setdefault(k, {})
            if ev[0] == "e":
                rd[ev[1]] = ev[2]
            else:
                rd.setdefault("_d", []).append(ev)
        for k in writes:
            self.last_write[k] = ev
            self.readers[k] = {}

    def op(self, eng, fn, reads=(), writes=(), sig=False):
        E = self.eng[eng]
        ps_r = [k for k in reads if isinstance(k, tuple) and k[0] in ("psf", "psb")]
        if ps_r:
            writes = list(writes) + ps_r
        for ev in self._deps(reads, writes):
            self._wait(E, ev)
        inst = fn(E["obj"])
        idx = len(E["insts"])
        E["insts"].append(inst)
        if sig or eng in ("act", "dve", "pool"):
            self._signal_count(E, idx)
        ev = ("e", eng, idx)
        self._record(ev, reads, writes)
        return ev

    def dma(self, queue, out, in_, sem, semstate, reads=(), writes=()):
        E = self.eng[queue]
        for ev in self._deps(reads, writes):
            self._wait(E, ev)
        E["obj"].dma_start(out=out, in_=in_).then_inc(sem, 16)
        semstate["v"] = semstate.get("v", 0) + 16
        ev = ("d", sem, semstate["v"])
        self._record(ev, reads, writes)
        return ev

    def barrier(self):
        names = ["pe", "act", "dve"]
        evs = {}
        for n in names:
            E = self.eng[n]
            if E["insts"]:
                evs[n] = ("e", n, len(E["insts"]) - 1)
        for n in names + ["pool", "sp"]:
            for m, ev in evs.items():
                if m != n:
                    self._wait(self.eng[n], ev)


def tiles_of(t0, n):
    return range(t0 // 128, (t0 + n - 1) // 128 + 1)


def build_program(stage=3):
    nc = bass.Bass("TRN2", target_bir_lowering=False)
    dt_in = lambda name, shape: nc.dram_tensor(name, shape, F32, kind="ExternalInput").ap()
    xin = dt_in("xin", [T, D])
    cst_d = dt_in("cst", [128, NCOL])
    msk_d = dt_in("msk", [128, 512])
    idn_d = dt_in("idn", [128, 128])
    w1g = dt_in("w1g", [D, DFF]); w1u = dt_in("w1u", [D, DFF]); w1d = dt_in("w1d", [DFF, D])
    w2g = dt_in("w2g", [D, DFF]); w2u = dt_in("w2u", [D, DFF]); w2d = dt_in("w2d", [DFF, D])
    win = dt_in("win", [D, INW]); wout = dt_in("wout", [D, D])
    y = nc.dram_tensor("y", [TOWN, D], F32, kind="ExternalOutput").ap()

    with ExitStack() as stack:
        tr = Tracker(nc, stack)
        sb = lambda name, shape, dt, st=stack: st.enter_context(nc.sbuf_tensor(name, shape, dt))

        xb = sb("xb", [128, KC, T], BF16)
        r = sb("r", [128, KC, T], F32)
        NW = 4
        wring = sb("wring", [128, NW, KC, 128], BF16)
        cst = sb("cst_sb", [128, NCOL], F32)
        cst2 = sb("cst2", [128, NCOL2], F32)
        msk = sb("msk_sb", [128, 512], F32)
        idf = sb("idf", [128, 128], F32)
        idb = sb("idb", [128, 128], BF16)
        avgD = sb("avgD", [128, 128], F32)
        avgC = sb("avgC", [128, 128], F32)
        ln_mean = sb("ln_mean", [128, 512], F32)
        ln_rstd = sb("ln_rstd", [128, 512], F32)
        ln_nmr = sb("ln_nmr", [128, 512], F32)
        ln_sq = sb("ln_sq", [128, 2, 512], F32R)
        avgDr = sb("avgDr", [128, 128], F32R)
        avgCr = sb("avgCr", [128, 128], F32R)
        ln_t = sb("ln_t", [128, 2, 512], F32)
        psf = [stack.enter_context(nc.psum_tensor(f"psf{i}", [128, 512], F32)) for i in range(7)]
        psb = [stack.enter_context(nc.psum_tensor(f"psb{i}", [128, 1024], BF16)) for i in range(1)]
        st = dict(pf=0, pb=0, sq=0, lt=0, ring=list(range(7)))

        def psum_f():
            ring = st["ring"]
            i = ring[st["pf"] % len(ring)]
            st["pf"] += 1
            return psf[i], ("psf", i)

        def psum_b():
            return psb[0], ("psb", 0)

        csem = tr.new_sem("cld"); cstate = {}
        tr.dma("sp", cst[:], cst_d[:, :], csem, cstate, writes=["cst"])
        tr.dma("sp", msk[:], msk_d[:, :], csem, cstate, writes=["msk"])
        tr.dma("sp", idf[:], idn_d[:, :], csem, cstate, writes=["idf"])
        for k_ in ("cst", "msk", "idf"):
            tr.last_write[k_] = ("d", csem, cstate["v"])
        tr.op("dve", lambda e: e.tensor_copy(idb[:], idf[:]), reads=["idf"], writes=["idb"])
        tr.op("dve", lambda e: e.memset(avgD[:], 1.0 / D), writes=["avgD"])
        tr.op("dve", lambda e: e.memset(avgC[:], 1.0 / 1024), writes=["avgC"])
        tr.op("dve", lambda e: e.tensor_copy(avgDr[:], avgD[:]), reads=["avgD"], writes=["avgDr"])
        tr.op("dve", lambda e: e.tensor_copy(avgCr[:], avgC[:]), reads=["avgC"], writes=["avgCr"])
        tr.op("dve", lambda e: e.memset(cst2[:, D_EPS:D_EPS + 1], EPS), writes=["cst2"])
        tr.op("dve", lambda e: e.tensor_scalar(cst2[:, 0:32], cst[:, C_LN1G:C_LN1G + 32], ALPHA, None, ALU.mult),
              reads=["cst"], writes=["cst2"])
        tr.op("dve", lambda e: e.tensor_scalar(cst2[:, 32:64], cst[:, C_LN2G:C_LN2G + 32], ALPHA, None, ALU.mult),
              reads=["cst"], writes=["cst2"])
        tr.op("dve", lambda e: e.tensor_scalar(cst2[:, D_NSINK:D_NSINK + 16], cst[:, C_SINK:C_SINK + 16], -1.0, None, ALU.mult),
              reads=["cst"], writes=["cst2"])
        tr.op("dve", lambda e: e.tensor_reduce(cst2[:, D_NMSINK:D_NMSINK + 4], cst[:, C_SINK:C_SINK + 16].rearrange("p (b h) -> p b h", h=4),
                                               AX.X, ALU.max), reads=["cst"], writes=["cst2"])
        tr.op("dve", lambda e: e.tensor_scalar(cst2[:, D_NMSINK:D_NMSINK + 4], cst2[:, D_NMSINK:D_NMSINK + 4], -1.0, None, ALU.mult),
              reads=["cst2"], writes=["cst2"])

        class WStream:
            def __init__(self, items, nslots, keyname, dst_fn, depth=None):
                self.items = items
                self.n = nslots
                self.key = keyname
                self.dst_fn = dst_fn
                self.issued = 0
                self.cur = 0
                self.sems = [tr.new_sem(keyname) for _ in range(nslots)]
                self.state = [dict() for _ in range(nslots)]
                self.depth = depth or nslots

            def _issue(self, i):
                s = i % self.n
                for src, sel in self.items[i]:
                    tr.dma("pool", self.dst_fn(s, sel), src, self.sems[s], self.state[s],
                           writes=[(self.key, s)])

            def prefetch(self, upto=None):
                while self.issued < min(len(self.items), self.cur + (upto or self.n)):
                    self._issue(self.issued)
                    self.issued += 1

            def get(self):
                i = self.cur
                while self.issued <= i:
                    self._issue(self.issued)
                    self.issued += 1
                self.cur += 1
                return i % self.n

        def colchunk(W, j):
            return [(W[:, j * 128:(j + 1) * 128].rearrange("(k p) f -> p k f", p=128), None)]

        def kdup(hk):
            src = win[:, 1024 + hk * 64:1024 + (hk + 1) * 64].rearrange("(k p) f -> p k f", p=128)
            return [(src, 0), (src, 1)]

        witems = []
        for j in range(NFC):
            witems.append(colchunk(w1g, j)); witems.append(colchunk(w1u, j))
        witems.append(kdup(0)); witems.append(kdup(1))
        witems.append(colchunk(win, 9))
        for j in range(8):
            witems.append(colchunk(win, j))
        for cc in range(8):
            witems.append(colchunk(win, 10 + cc)); witems.append(colchunk(win, 18 + cc))
        for half in range(2):
            for kk in range(KC):
                witems.append([(wout[half * 1024:(half + 1) * 1024, kk * 128:(kk + 1) * 128].rearrange("(k p) f -> p k f", p=128), "h8")])
        for j in range(NFC):
            witems.append(colchunk(w2g, j)); witems.append(colchunk(w2u, j))

        def wdst(s, sel):
            if sel is None:
                return wring[:, s]
            if sel == "h8":
                return wring[:, s, 0:8]
            return wring[:, s, :, sel * 64:(sel + 1) * 64]

        ws = WStream(witems, NW, "w", wdst)

        def xkeys(name, k, t0, n):
            return [(name, k, t) for t in tiles_of(t0, n)]

        def proj_group(slot, t0, n, out_ps, out_key, src_fn=None, nk=KC):
            if src_fn is None:
                src_fn = lambda k, t0, n: (xb[:, k, t0:t0 + n], xkeys("xb", k, t0, n))
            for k in range(nk):
                ap, keys = src_fn(k, t0, n)
                tr.op("pe", lambda e: e.matmul(out_ps[:, 0:n], wring[:, slot, k, :], ap,
                                               start=(k == 0), stop=(k == nk - 1)),
                      reads=[("w", slot)] + keys, writes=[out_key], sig=(k == nk - 1))

        def layernorm(nch, src_fn, src_keys_fn, groups, avg, emit_out, hook=None, after_group=None, stat_banks=None):
            avgr = avgDr if avg is avgD else avgCr

            sbi = [0]

            def stat_bank():
                if stat_banks is None:
                    return psum_f()
                i_ = stat_banks[sbi[0] % len(stat_banks)]; sbi[0] += 1
                return psf[i_], ("psf", i_)

            def stats(t0, n):
                p1, k1 = stat_bank()
                p2, k2 = stat_bank()
                for k in range(nch):
                    sqi = st["sq"] % 2; st["sq"] += 1
                    sk = src_keys_fn(k, t0, n)
                    tr.op("act", lambda e: e.activation(ln_sq[:, sqi, 0:n], src_fn(k, t0, n), AF.Square),
                          reads=sk, writes=[("lnsq", sqi)])
                    tr.op("pe", lambda e: e.matmul(p1[:, 0:n], avg[:], src_fn(k, t0, n),
                                                   start=(k == 0), stop=(k == nch - 1)),
                          reads=sk + ["avg"], writes=[k1])
                    tr.op("pe", lambda e: e.matmul(p2[:, 0:n], avgr[:], ln_sq[:, sqi, 0:n],
                                                   start=(k == 0), stop=(k == nch - 1)),
                          reads=[("lnsq", sqi), "avg"], writes=[k2], sig=True)
                return p1, k1, p2, k2

            def finalize(t0, n, p1, k1, p2, k2):
                tr.op("dve", lambda e: e.tensor_copy(ln_mean[:, 0:n], p1[:, 0:n]), reads=[k1], writes=["lnmean"])
                tr.op("dve", lambda e: e.tensor_tensor(ln_rstd[:, 0:n], ln_mean[:, 0:n], ln_mean[:, 0:n], ALU.mult),
                      reads=["lnmean"], writes=["lnrstd"])
                tr.op("dve", lambda e: e.tensor_tensor(ln_rstd[:, 0:n], p2[:, 0:n], ln_rstd[:, 0:n], ALU.subtract),
                      reads=[k2, "lnrstd"], writes=["lnrstd"])
                tr.op("act", lambda e: e.activation(ln_rstd[:, 0:n], ln_rstd[:, 0:n], AF.Sqrt,
                                                    bias=cst2[:, D_EPS:D_EPS + 1], scale=1.0),
                      reads=["lnrstd", "cst2"], writes=["lnrstd"])
                tr.op("dve", lambda e: e.reciprocal(ln_rstd[:, 0:n], ln_rstd[:, 0:n]), reads=["lnrstd"], writes=["lnrstd"])
                tr.op("dve", lambda e: e.scalar_tensor_tensor(ln_nmr[:, 0:n], ln_mean[:, 0:n], -1.0, ln_rstd[:, 0:n],
                                                              ALU.mult, ALU.mult),
                      reads=["lnmean", "lnrstd"], writes=["lnnmr"])

            def normalize(t0, n):
                for k in range(nch):
                    ti = st["lt"] % 2; st["lt"] += 1
                    sk = src_keys_fn(k, t0, n)
                    tr.op("dve", lambda e: e.tensor_tensor(ln_t[:, ti, 0:n], src_fn(k, t0, n), ln_rstd[:, 0:n], ALU.mult),
                          reads=sk + ["lnrstd"], writes=[("lnt", ti)])
                    tr.op("dve", lambda e: e.tensor_tensor(ln_t[:, ti, 0:n], ln_t[:, ti, 0:n], ln_nmr[:, 0:n], ALU.add),
                          reads=[("lnt", ti), "lnnmr"], writes=[("lnt", ti)])
                    emit_out(k, t0, n, ln_t[:, ti, 0:n], ("lnt", ti))
                    if hook is not None:
                        hook()

            pend = stats(*groups[0])
            for gi, (t0, n) in enumerate(groups):
                nxt = stats(*groups[gi + 1]) if gi + 1 < len(groups) else None
                finalize(t0, n, *pend)
                normalize(t0, n)
                if after_group is not None:
                    after_group(t0, n)
                pend = nxt

        def rsrc(k, t0, n):
            return r[:, k, t0:t0 + n]

        def rkeys(k, t0, n):
            return xkeys("r", k, t0, n)

        def ln_main_out(gcol, bcol, agcol, abcol, final=False):
            def emit(k, t0, n, tap, tkey):
                if not final:
                    tr.op("act", lambda e: e.activation(xb[:, k, t0:t0 + n], tap, AF.Identity,
                                                        bias=cst[:, bcol + k:bcol + k + 1], scale=cst[:, gcol + k:gcol + k + 1]),
                          reads=[tkey, "cst"], writes=xkeys("xb", k, t0, n))
                    tr.op("act", lambda e: e.activation(r[:, k, t0:t0 + n], tap, AF.Identity,
                                                        bias=cst2[:, abcol + k:abcol + k + 1], scale=cst2[:, agcol + k:agcol + k + 1]),
                          reads=[tkey, "cst2"], writes=xkeys("r", k, t0, n))
                elif k % 2 == 0:
                    tr.op("act", lambda e: e.activation(r[:, k, t0:t0 + n], tap, AF.Identity,
                                                        bias=cst[:, bcol + k:bcol + k + 1], scale=cst[:, gcol + k:gcol + k + 1]),
                          reads=[tkey, "cst"], writes=xkeys("r", k, t0, n))
                else:
                    tr.op("act", lambda e: e.activation(r[:, k, t0:t0 + n], tap, AF.Identity,
                                                        bias=cst[:, bcol + k:bcol + k + 1], scale=cst[:, gcol + k:gcol + k + 1]),
                          reads=[tkey, "cst"], writes=xkeys("r", k, t0, n))
            return emit

        def ffn(Wd, groups, tagn):
            with ExitStack() as ph:
                hT = sb(f"hT{tagn}", [128, FB, T], BF16, ph)
                sil = sb(f"sil{tagn}", [128, 2, 512], F32, ph)
                dring = sb(f"dring{tagn}", [128, 2, FB, 512], BF16, ph)
                ditems = []
                for fb in range(NFB):
                    for dq in range(4):
                        src = Wd[fb * FB * 128:(fb + 1) * FB * 128, dq * 512:(dq + 1) * 512].rearrange("(c p) d -> p c d", p=128)
                        ditems.append([(src, None)])
                ds = WStream(ditems, 2, f"d{tagn}", lambda s, sel: dring[:, s])
                si = 0
                for fb in range(NFB):
                    for c in range(FB):
                        if c == 3:
                            ds.prefetch()
                        sg = ws.get(); su = ws.get()
                        for (t0, n) in groups:
                            pg, kg = psum_f(); pu, ku = psum_f()
                            proj_group(sg, t0, n, pg, kg)
                            proj_group(su, t0, n, pu, ku)
                            s_i = si % 2; si += 1
                            tr.op("act", lambda e, s_i=s_i, pg=pg: e.activation(sil[:, s_i, 0:n], pg[:, 0:n], AF.Silu),
                                  reads=[kg], writes=[("sil", s_i)])
                            tr.op("dve", lambda e, s_i=s_i, pu=pu, c=c: e.tensor_tensor(hT[:, c, t0:t0 + n], sil[:, s_i, 0:n], pu[:, 0:n], ALU.mult),
                                  reads=[("sil", s_i), ku], writes=xkeys("h", c, t0, n))
                        ws.prefetch()
                    for dq in range(4):
                        sd = ds.get()
                        for dk in range(4):
                            kk = dq * 4 + dk
                            for (t0, n) in groups:
                                pd, kd = psum_f()
                                for c in range(FB):
                                    tr.op("pe", lambda e, c=c, pd=pd: e.matmul(pd[:, 0:n], dring[:, sd, c, dk * 128:(dk + 1) * 128],
                                                                              hT[:, c, t0:t0 + n], start=(c == 0), stop=(c == FB - 1)),
                                          reads=[(f"d{tagn}", sd)] + xkeys("h", c, t0, n), writes=[kd], sig=(c == FB - 1))
                                tr.op("dve", lambda e, pd=pd, kk=kk: e.scalar_tensor_tensor(r[:, kk, t0:t0 + n], pd[:, 0:n], 0.5, r[:, kk, t0:t0 + n],
                                                                                            ALU.mult, ALU.add),
                                      reads=[kd] + xkeys("r", kk, t0, n), writes=xkeys("r", kk, t0, n))
                        ds.prefetch()
                tr.barrier()

        ws.prefetch(upto=2)
        with ExitStack() as ph:
            NXT = 4
            xt = sb("xt", [128, NXT, D], F32, ph)
            xsem = [tr.new_sem("xl") for _ in range(NXT)]
            xstate = [dict() for _ in range(NXT)]
            for i in range(int(os.environ.get('K_NT0', T // 128))):
                s = i % NXT
                tr.dma("sp", xt[:, s], xin[i * 128:(i + 1) * 128, :], xsem[s], xstate[s], writes=[("xt", s)])
                for kq in range(4):
                    pt, kt = psum_f()
                    for c in range(4):
                        k = kq * 4 + c
                        tr.op("pe", lambda e, k=k, c=c, pt=pt: e.transpose(pt[:, c * 128:(c + 1) * 128], xt[:, s, k * 128:(k + 1) * 128], idf[:]),
                              reads=[("xt", s), "idf"], writes=[kt], sig=(c == 3))
                    pv = pt[:, :].rearrange("p (c t) -> p c t", c=4)
                    tr.op("act", lambda e, pv=pv, kq=kq: e.activation(r[:, kq * 4:(kq + 1) * 4, i * 128:(i + 1) * 128], pv, AF.Identity, scale=ALPHA),
                          reads=[kt], writes=[("r", kq * 4 + c, i) for c in range(4)])
                    tr.op("dve", lambda e, pv=pv, kq=kq: e.tensor_copy(xb[:, kq * 4:(kq + 1) * 4, i * 128:(i + 1) * 128], pv),
                          reads=[kt], writes=[("xb", kq * 4 + c, i) for c in range(4)])
            tr.barrier()
        ws.prefetch()

        G3 = [(0, 384), (384, 384), (768, 384)]
        G2 = [(128, 512), (640, 512)]

        if stage >= 1:
            if not os.environ.get("K_SKIP_FFN"):
                ffn(w1d, G3, 1)
            if not os.environ.get("K_NO_LN"):
                layernorm(KC, rsrc, rkeys, G3, avgD, ln_main_out(C_LN1G, C_LN1B, D_ALN1G, D_ALN1B))

        def mixer():
            with ExitStack() as m1:
                qT = sb("qT", [128, 8, TOWN], BF16, m1)
                kd_ = sb("kdup", [128, 2, T], BF16, m1)
                vv = sb("vv", [128, 9, 128], BF16, m1)
                hall = sb("hall", [128, 8, T], BF16, m1)
                dg = sb("dgring", [128, 16, 128], BF16, m1)
                sg_t = sb("sgt", [128, 2, 384], F32, m1)
                stt = sb("stt", [128, 4, 32], F32, m1)
                acc = xb[:].rearrange("p k t -> p (k t)").bitcast(F32)[:, 0:8 * TOWN].rearrange("p (c t) -> p c t", c=8)
                pT1 = sb("pT1", [128, 8, 128], BF16, m1)
                ao2 = sb("ao", [128, 2, 1024], BF16, m1)
                s0 = sb("s_buf0", [128, 4, 256], F32, m1)
                s_buf = [s0[:], ln_t[:].rearrange("p a (h k) -> p (a h) k", h=2)]
                s_key = [["s_buf0"], [("lnt", 0), ("lnt", 1)]]
                e_buf = [ln_mean[:].bitcast(BF16).rearrange("p (h k) -> p h k", h=4),
                         ln_rstd[:].bitcast(BF16).rearrange("p (h k) -> p h k", h=4)]
                e_key = ["lnmean", "lnrstd"]
                pT_buf = [ln_nmr[:].bitcast(BF16).rearrange("p (a b) -> p a b", a=8), pT1[:]]
                pT_key = ["lnnmr", "pT1"]

                for hk in range(2):
                    s_ = ws.get()
                    for (t0, n) in G3:
                        p, kp = psum_f()
                        proj_group(s_, t0, n, p, kp)
                        tr.op("act", lambda e: e.activation(kd_[:, hk, t0:t0 + n], p[:, 0:n], AF.Identity,
                                                            bias=cst[:, C_BKD + hk:C_BKD + hk + 1], scale=1.0),
                              reads=[kp, "cst"], writes=[("kd", hk, t) for t in tiles_of(t0, n)])
                    ws.prefetch()
                s_ = ws.get()
                for i in range(9):
                    p, kp = psum_f()
                    for k in range(KC):
                        tr.op("pe", lambda e: e.matmul(p[:, 0:128], xb[:, k, i * 128:(i + 1) * 128], wring[:, s_, k, :],
                                                       start=(k == 0), stop=(k == KC - 1)),
                              reads=[("w", s_), ("xb", k, i)], writes=[kp], sig=(k == KC - 1))
                    tr.op("dve", lambda e: e.tensor_tensor(vv[:, i, :], p[:, 0:128], cst[:, C_BV:C_BV + 128], ALU.add),
                          reads=[kp, "cst"], writes=[("v", i)])
                ws.prefetch()
                for j in range(8):
                    s_ = ws.get()
                    for (t0, n) in G2:
                        p, kp = psum_f()
                        proj_group(s_, t0, n, p, kp)
                        tr.op("act", lambda e: e.activation(qT[:, j, t0 - 128:t0 - 128 + n], p[:, 0:n], AF.Identity,
                                                            bias=cst[:, C_BIN + j:C_BIN + j + 1], scale=1.0),
                              reads=[kp, "cst"], writes=[("q", j, t) for t in tiles_of(t0, n)])
                    ws.prefetch()
                sgi = 0
                GC = [(98, 286), (384, 384), (768, 384)]
                for cc in range(8):
                    sa = ws.get(); sgt = ws.get()
                    for (t0, n) in GC:
                        pa, ka = psum_f(); pg, kg = psum_f()
                        proj_group(sa, t0, n, pa, ka)
                        proj_group(sgt, t0, n, pg, kg)
                        s_i = sgi % 2; sgi += 1
                        tr.op("act", lambda e: e.activation(sg_t[:, s_i, 0:n], pg[:, 0:n], AF.Sigmoid,
                                                            bias=cst[:, C_BIN + 18 + cc:C_BIN + 19 + cc], scale=1.0),
                              reads=[kg, "cst"], writes=[("sg", s_i)])
                        tr.op("dve", lambda e: e.scalar_tensor_tensor(hall[:, cc, t0:t0 + n], pa[:, 0:n], cst[:, C_BIN + 10 + cc:C_BIN + 11 + cc],
                                                                      sg_t[:, s_i, 0:n], ALU.add, ALU.mult),
                              reads=[ka, ("sg", s_i), "cst"], writes=[("hglu", cc)])
                    tr.op("dve", lambda e: e.tensor_scalar(hall[:, cc, 98:128], hall[:, cc, 98:128], cst[:, C_FLAG:C_FLAG + 1], None, ALU.mult),
                          reads=[("hglu", cc), "cst"], writes=[("hglu", cc)])
                    ws.prefetch()

                SA, SB, PO = 4, 5, 6
                st["ring"] = [0, 1, 2, 3]
                NB = 32

                def tb(t):
                    return 1 + t // 4, t % 4

                def S1(t):
                    i, b = tb(t)
                    hk = b // 2
                    mcol = 0 if i == 1 else 256
                    for hq in range(4):
                        h = 4 * b + hq
                        j, half = h // 2, h % 2
                        bank = SA if hq % 2 == 0 else SB
                        tr.op("pe", lambda e: e.matmul(
                            psf[bank][:, (hq // 2) * 256:(hq // 2) * 256 + 256],
                            qT[half * 64:(half + 1) * 64, j, (i - 1) * 128:i * 128],
                            kd_[half * 64:(half + 1) * 64, hk, (i - 1) * 128:(i + 1) * 128], start=True, stop=True),
                            reads=[("q", j, i), ("kd", hk, i - 1), ("kd", hk, i)], writes=[("psf", bank)], sig=(hq >= 2))
                    for x, bank in enumerate((SA, SB)):
                        tr.op("dve", lambda e: e.scalar_tensor_tensor(
                            s_buf[t % 2][:, 2 * x:2 * x + 2, :], psf[bank][:, :].rearrange("p (h k) -> p h k", h=2), 0.125,
                            msk[:, mcol:mcol + 256].unsqueeze(1).to_broadcast([128, 2, 256]), ALU.mult, ALU.add),
                            reads=[("psf", bank), "msk"], writes=s_key[t % 2])

                def S2(t):
                    i, b = tb(t)
                    sv = stt[:, t % 4, :]
                    sk = ("stt", t % 4)
                    sb_ = s_buf[t % 2]
                    tr.op("dve", lambda e: e.tensor_reduce(sv[:, 0:4], sb_, AX.X, ALU.max), reads=s_key[t % 2], writes=[sk])
                    tr.op("dve", lambda e: e.scalar_tensor_tensor(sv[:, 4:8], sv[:, 0:4], -1.0, cst2[:, D_NSINK + 4 * b:D_NSINK + 4 * b + 4],
                                                                  ALU.mult, ALU.min),
                          reads=[sk, "cst2"], writes=[sk])
                    tr.op("dve", lambda e: e.tensor_tensor(sv[:, 12:16], cst[:, C_SINK + 4 * b:C_SINK + 4 * b + 4], sv[:, 4:8], ALU.add),
                          reads=[sk, "cst"], writes=[sk])
                    tr.op("dve", lambda e: e.memset(sv[:, 8:12], 0.0), writes=[sk])
                    for p_ in range(4):
                        tr.op("act", lambda e: e.activation(e_buf[t % 2][:, p_, :], sb_[:, p_, :], AF.Exp,
                                                            bias=sv[:, 4 + p_:5 + p_], scale=1.0, accum_out=sv[:, 8 + p_:9 + p_]),
                              reads=s_key[t % 2] + [sk], writes=[e_key[t % 2], sk])
                    tr.op("act", lambda e: e.activation(sv[:, 16:20], sv[:, 12:16], AF.Exp), reads=[sk], writes=[sk])

                def S3(t):
                    pb, kb = psum_b()
                    for p_ in range(4):
                        for blk in range(2):
                            tr.op("pe", lambda e: e.transpose(pb[:, (p_ * 2 + blk) * 128:(p_ * 2 + blk + 1) * 128],
                                                              e_buf[t % 2][:, p_, blk * 128:(blk + 1) * 128], idb[:]),
                                  reads=[e_key[t % 2], "idb"], writes=[kb], sig=(p_ == 3 and blk == 1))
                    tr.op("act", lambda e: e.activation(pT_buf[t % 2].rearrange("p a b -> p (a b)"), pb[:, :], AF.Copy),
                          reads=[kb], writes=[pT_key[t % 2]])

                def S4(t):
                    i, b = tb(t)
                    hk = b // 2
                    sv = stt[:, t % 4, :]
                    sk = ("stt", t % 4)
                    po = psf[PO]
                    ao = ao2[:, i % 2, :]
                    aok = ("ao", i % 2)
                    for p_ in range(4):
                        for blk in range(2):
                            tr.op("pe", lambda e: e.matmul(po[:, p_ * 64:(p_ + 1) * 64], pT_buf[t % 2][:, p_ * 2 + blk, :],
                                                           vv[:, i - 1 + blk, hk * 64:(hk + 1) * 64], start=(blk == 0), stop=(blk == 1)),
                                  reads=[pT_key[t % 2], ("v", i - 1 + blk)], writes=[("psf", PO)], sig=(p_ == 3 and blk == 1))
                    tr.op("dve", lambda e: e.tensor_tensor(sv[:, 16:20], sv[:, 16:20], sv[:, 8:12], ALU.add), reads=[sk], writes=[sk])
                    tr.op("dve", lambda e: e.reciprocal(sv[:, 16:20], sv[:, 16:20]), reads=[sk], writes=[sk])
                    tr.op("dve", lambda e: e.tensor_tensor(
                        ao[:, b * 256:(b + 1) * 256].rearrange("p (a two d) -> p two a d", two=2, d=64),
                        po[:, 0:256].rearrange("p (two a d) -> p two a d", two=2, a=2),
                        sv[:, 16:20].rearrange("p (two a) -> p two a", two=2).unsqueeze(3).to_broadcast([128, 2, 2, 64]), ALU.mult),
                        reads=[("psf", PO), sk], writes=[aok])
                    if b == 3:
                        def tail():
                            pb, kb = psum_b()
                            for c in range(8):
                                tr.op("pe", lambda e: e.transpose(pb[:, c * 128:(c + 1) * 128], ao[:, c * 128:(c + 1) * 128], idb[:]),
                                      reads=[aok, "idb"], writes=[kb], sig=(c == 7))
                            tr.op("act", lambda e: e.activation(qT[:, 0:8, (i - 1) * 128:i * 128],
                                                                pb[:, :].rearrange("p (c t) -> p c t", c=8), AF.Copy),
                                  reads=[kb], writes=[("q", c, i) for c in range(8)])
                        deferred.append(tail)

                deferred = []

                def attn_iter(it):
                    for stage_fn, t in ((S2, it + 2), (S1, it + 3), (S3, it + 1), (S4, it)):
                        if 0 <= t < NB:
                            stage_fn(t)

                SEG = 7
                NDG = 16
                taps = [(cc, jt) for cc in range(8) for jt in range(31)]
                segs = [taps[i_:i_ + SEG] for i_ in range(0, len(taps), SEG)]
                dslot = {}

                def gen_diags(seg):
                    for (cc, jt) in seg:
                        di = len(dslot) % NDG
                        dslot[(cc, jt)] = di
                        if len(dslot) % SEG in (2, 4, 6):
                            tr.op("act", lambda e: e.activation(dg[:, di, :], idb[:], AF.Identity,
                                                                scale=cst[:, C_CW + jt * 8 + cc:C_CW + jt * 8 + cc + 1]),
                                  reads=["idb", "cst"], writes=[("dg", di)])
                        else:
                            tr.op("dve", lambda e: e.tensor_scalar(dg[:, di, :], idb[:], cst[:, C_CW + jt * 8 + cc:C_CW + jt * 8 + cc + 1], None, ALU.mult),
                                  reads=["idb", "cst"], writes=[("dg", di)])

                it = -3
                pcs = None
                gen_diags(segs[0])
                for si, seg in enumerate(segs):
                    if si + 1 < len(segs):
                        gen_diags(segs[si + 1])
                    if it < NB:
                        attn_iter(it); it += 1
                    for (cc, jt) in seg:
                        if jt == 0:
                            pcs = [psum_f(), psum_f()]
                        di = dslot[(cc, jt)]
                        for gi in range(2):
                            pc, kc = pcs[gi]
                            o0 = 98 + jt + gi * 512
                            tr.op("pe", lambda e: e.matmul(pc[:, 0:512], dg[:, di, :], hall[:, cc, o0:o0 + 512],
                                                           start=(jt == 0), stop=(jt == 30)),
                                  reads=[("dg", di), ("hglu", cc)], writes=[kc], sig=(jt == 30))
                        if jt == 30:
                            for gi in range(2):
                                pc, kc = pcs[gi]
                                tr.op("act", lambda e: e.activation(acc[:, cc, gi * 512:(gi + 1) * 512], pc[:, 0:512], AF.Identity,
                                                                    bias=cst[:, C_CB + cc:C_CB + cc + 1], scale=1.0),
                                      reads=[kc, "cst"], writes=[("acc", cc)])
                    while deferred:
                        deferred.pop(0)()
                while it < NB:
                    attn_iter(it); it += 1
                    while deferred:
                        deferred.pop(0)()
                st["ring"] = list(range(7))

                def outproj_unit(kk, half, slot, t0, n):
                    p, kp = psum_f()
                    for k in range(8):
                        if half == 0:
                            ap, keys = qT[:, k, t0 - 128:t0 - 128 + n], [("q", k, t) for t in tiles_of(t0, n)]
                        else:
                            ap, keys = hall[:, k, t0 - 128:t0 - 128 + n], [("hglu", k)]
                        tr.op("pe", lambda e: e.matmul(p[:, 0:n], wring[:, slot, k, :], ap, start=(k == 0), stop=(k == 7)),
                              reads=[("w", slot)] + keys, writes=[kp], sig=(k == 7))
                    if half == 0:
                        tr.op("dve", lambda e: e.tensor_tensor(r[:, kk, t0:t0 + n], p[:, 0:n], r[:, kk, t0:t0 + n], ALU.add),
                              reads=[kp] + xkeys("r", kk, t0, n), writes=xkeys("r", kk, t0, n))
                    else:
                        tr.op("dve", lambda e: e.scalar_tensor_tensor(r[:, kk, t0:t0 + n], p[:, 0:n], cst[:, C_BOUT + kk:C_BOUT + kk + 1],
                                                                      r[:, kk, t0:t0 + n], ALU.add, ALU.add),
                              reads=[kp, "cst"] + xkeys("r", kk, t0, n), writes=xkeys("r", kk, t0, n))

                def outproj_chunk(kk, half):
                    s_ = ws.get()
                    for (t0, n) in G2:
                        outproj_unit(kk, half, s_, t0, n)
                    ws.prefetch()

                pass1 = list(range(KC))

                def hook():
                    if pass1:
                        outproj_chunk(pass1.pop(0), 0)

                def conv_out(k, t0, n, tap, tkey):
                    tr.op("act", lambda e: e.activation(hall[:, k, t0:t0 + n], tap, AF.Silu,
                                                        bias=cst[:, C_CLB + k:C_CLB + k + 1], scale=cst[:, C_CLG + k:C_CLG + k + 1]),
                          reads=[tkey, "cst"], writes=[("hglu", k)])
                st["ring"] = [0, 1, 2]
                layernorm(8, lambda k, t0, n: acc[:, k, t0:t0 + n], lambda k, t0, n: [("acc", k)],
                          [(0, 512), (512, 512)], avgC, conv_out, hook=hook, stat_banks=[3, 4, 5, 6])
                st["ring"] = list(range(7))
                while pass1:
                    hook()
                for kk in range(KC):
                    outproj_chunk(kk, 1)
                tr.barrier()
            layernorm(KC, rsrc, rkeys, G2, avgD, ln_main_out(C_LN2G, C_LN2B, D_ALN2G, D_ALN2B))

        if stage >= 2:
            mixer()

        ffn(w2d, G2, 2)
        with ExitStack() as ph:
            NOT = 4
            ot = sb("ot", [128, NOT, D], F32, ph)
            osem = [tr.new_sem("ost") for _ in range(NOT)]
            ostate = [dict() for _ in range(NOT)]

            def emit_output(t0, n):
                for i in tiles_of(t0, n):
                    s = i % NOT
                    for kq in range(4):
                        pt, kt = psum_f()
                        for c in range(4):
                            k = kq * 4 + c
                            tr.op("pe", lambda e: e.transpose(pt[:, c * 128:(c + 1) * 128], r[:, k, i * 128:(i + 1) * 128], idf[:]),
                                  reads=[("r", k, i), "idf"], writes=[kt], sig=(c == 3))
                        tr.op("act", lambda e: e.activation(ot[:, s, kq * 512:(kq + 1) * 512], pt[:, :], AF.Copy),
                              reads=[kt], writes=[("ot", s, kq)])
                    tr.dma("sp", y[(i - 1) * 128:i * 128, :], ot[:, s], osem[s], ostate[s],
                           reads=[("ot", s, kq) for kq in range(4)])

            st["ring"] = [0, 1, 2]
            layernorm(KC, rsrc, rkeys, G2, avgD, ln_main_out(C_LN3G, C_LN3B, 0, 0, final=True), after_group=emit_output,
                      stat_banks=[3, 4, 5, 6])
            for s in range(NOT):
                nc.sync.wait_ge(osem[s], ostate[s]["v"])
    return nc


_CACHE = {}


def _host_consts(inp, half):
    c = np.zeros((128, NCOL), np.float32)
    fm = lambda v: np.asarray(v, np.float32).reshape(-1, 128).T
    c[:, C_LN1G:C_LN1G + 16] = fm(inp["ln1_g"][0]); c[:, C_LN1B:C_LN1B + 16] = fm(inp["ln1_b"][0])
    c[:, C_LN2G:C_LN2G + 16] = fm(inp["ln2_g"][0]); c[:, C_LN2B:C_LN2B + 16] = fm(inp["ln2_b"][0])
    c[:, C_LN3G:C_LN3G + 16] = fm(inp["ln3_g"][0]); c[:, C_LN3B:C_LN3B + 16] = fm(inp["ln3_b"][0])
    c[:, C_BIN:C_BIN + 26] = fm(inp["b_in"][0])
    c[:, C_BOUT:C_BOUT + 16] = fm(inp["b_out"][0])
    cw = np.asarray(inp["conv_dw_w"][0], np.float32)
    for j in range(31):
        c[:, C_CW + j * 8:C_CW + j * 8 + 8] = fm(cw[j])
    c[:, C_CB:C_CB + 8] = fm(inp["conv_dw_b"][0])
    c[:, C_CLG:C_CLG + 8] = fm(inp["conv_ln_g"][0]); c[:, C_CLB:C_CLB + 8] = fm(inp["conv_ln_b"][0])
    sk = np.asarray(inp["attn_sinks"][0], np.float32).reshape(4, 4)[:, [0, 2, 1, 3]].reshape(16)
    c[:, C_SINK:C_SINK + 16] = sk[None, :]
    bin_ = np.asarray(inp["b_in"][0], np.float32)
    c[:, C_BV:C_BV + 128] = bin_[1152:1280][None, :]
    c[:, C_FLAG] = float(half)
    for hk in range(2):
        bk = bin_[1024 + hk * 64:1024 + (hk + 1) * 64]
        c[:, C_BKD + hk] = np.concatenate([bk, bk])
    return c


def _masks(half):
    qi = np.arange(128)[:, None]
    kj = np.arange(256)[None, :]
    valid = (kj >= qi + 1) & (kj <= qi + 128)
    mb = np.where(valid, 0.0, NEG).astype(np.float32)
    ma = mb.copy()
    if half == 0:
        ma[:, :128] = NEG
    return np.concatenate([ma, mb], axis=1)


def make_in_maps(inp):
    x = np.ascontiguousarray(inp["x"], np.float32)
    shared = {
        "w1g": np.ascontiguousarray(inp["ffn1_w_gate"][0], np.float32),
        "w1u": np.ascontiguousarray(inp["ffn1_w_up"][0], np.float32),
        "w1d": np.ascontiguousarray(inp["ffn1_w_down"][0], np.float32),
        "w2g": np.ascontiguousarray(inp["ffn2_w_gate"][0], np.float32),
        "w2u": np.ascontiguousarray(inp["ffn2_w_up"][0], np.float32),
        "w2d": np.ascontiguousarray(inp["ffn2_w_down"][0], np.float32),
        "win": np.ascontiguousarray(inp["w_in"][0], np.float32),
        "wout": np.ascontiguousarray(inp["w_out"][0], np.float32),
        "idn": np.eye(128, dtype=np.float32),
    }
    in_maps = []
    for c in range(8):
        b, half = c // 2, c % 2
        if half == 0:
            xi = np.concatenate([np.zeros((128, D), np.float32), x[b, 0:1024]], axis=0)
        else:
            xi = x[b, 896:2048]
        m = dict(shared)
        m["xin"] = np.ascontiguousarray(xi)
        m["cst"] = _host_consts(inp, half)
        m["msk"] = _masks(half)
        in_maps.append(m)
    return in_maps


def kernel(**inputs):
    inp = {k: np.asarray(v) for k, v in inputs.items()}
    nc = build_program()
    in_maps = make_in_maps(inp)
    res = run_bass_kernel_spmd(nc, in_maps, core_ids=list(range(8)))
    out = np.empty((4, 2048, D), np.float32)
    for c in range(8):
        b, half = c // 2, c % 2
        out[b, half * 1024:(half + 1) * 1024] = res.results[c]["y"]
    return out
```

```python
import bisect
import os
from contextlib import ExitStack

import numpy as np
import concourse.bass as bass
import concourse.mybir as mybir
from concourse.bass_utils import run_bass_kernel_spmd

F32 = mybir.dt.float32
F32R = mybir.dt.float32r
BF16 = mybir.dt.bfloat16
AF = mybir.ActivationFunctionType
ALU = mybir.AluOpType
AX = mybir.AxisListType

D = 2048
DFF = 5632
NFC = DFF // 128
KC = D // 128
T = 1152
TOWN = 1024
INW = 3328
ALPHA = 2.0 ** 0.25
EPS = 1e-5
NEG = -30000.0
FB = 11
NFB = NFC // FB

C_LN1G, C_LN1B, C_LN2G, C_LN2B, C_LN3G, C_LN3B = 0, 16, 32, 48, 64, 80
C_BIN = 96
C_BOUT = 122
C_CW = 138
C_CB = 386
C_CLG = 394
C_CLB = 402
C_SINK = 410
C_BV = 426
C_FLAG = 554
C_BKD = 555
NCOL = 557
D_ALN1G, D_ALN1B, D_ALN2G, D_ALN2B, D_EPS, D_NSINK, D_NMSINK = 0, 16, 32, 48, 64, 65, 81
NCOL2 = 85

SAME_ENGINE_SYNC = True


class Tracker:
    def __init__(self, nc, stack):
        self.nc = nc
        self.stack = stack
        self.eng = {}
        for name, obj in (("pe", nc.tensor), ("act", nc.scalar), ("dve", nc.vector),
                          ("pool", nc.gpsimd), ("sp", nc.sync)):
            sem = stack.enter_context(nc.semaphore("prog_" + name))
            self.eng[name] = dict(name=name, obj=obj, sem=sem, insts=[], sig_idx=[], sig_cnt=[],
                                  waited={})
        self.last_write = {}
        self.readers = {}
        self.nsem = 0

    def new_sem(self, name):
        self.nsem += 1
        return self.stack.enter_context(self.nc.semaphore(f"{name}_{self.nsem}"))

    def _signal_count(self, E, idx):
        pos = bisect.bisect_left(E["sig_idx"], idx)
        if pos < len(E["sig_idx"]):
            return E["sig_cnt"][pos]
        last = len(E["insts"]) - 1
        E["insts"][last].then_inc(E["sem"], 1)
        cnt = len(E["sig_idx"]) + 1
        E["sig_idx"].append(last)
        E["sig_cnt"].append(cnt)
        return cnt

    def _wait(self, E, ev):
        if ev is None:
            return
        if ev[0] == "e":
            P = self.eng[ev[1]]
            if P is E:
                if E["name"] in ("pe", "sp", "pool") or not SAME_ENGINE_SYNC:
                    return
            cnt = self._signal_count(P, ev[2])
            sem = P["sem"]
        else:
            sem, cnt = ev[1], ev[2]
        key = id(sem)
        if E["waited"].get(key, 0) >= cnt:
            return
        E["obj"].wait_ge(sem, cnt)
        E["waited"][key] = cnt

    def _deps(self, reads, writes):
        deps = []
        for k in reads:
            w = self.last_write.get(k)
            if w is not None:
                deps.append(w)
        for k in writes:
            w = self.last_write.get(k)
            if w is not None:
                deps.append(w)
            rd = self.readers.get(k)
            if rd:
                for en, v in rd.items():
                    if en == "_d":
                        deps.extend(v)
                    else:
                        deps.append(("e", en, v))
        return deps

    def _record(self, ev, reads, writes):
        for k in reads:
            rd = self.readers.setdefault(k, {})
            if ev[0] == "e":
                rd[ev[1]] = ev[2]
            else:
                rd.setdefault("_d", []).append(ev)
        for k in writes:
            self.last_write[k] = ev
            self.readers[k] = {}

    def op(self, eng, fn, reads=(), writes=(), sig=False):
        E = self.eng[eng]
        ps_r = [k for k in reads if isinstance(k, tuple) and k[0] in ("psf", "psb")]
        if ps_r:
            writes = list(writes) + ps_r
        for ev in self._deps(reads, writes):
            self._wait(E, ev)
        inst = fn(E["obj"])
        idx = len(E["insts"])
        E["insts"].append(inst)
        if sig or eng in ("act", "dve", "pool"):
            self._signal_count(E, idx)
        ev = ("e", eng, idx)
        self._record(ev, reads, writes)
        return ev

    def dma(self, queue, out, in_, sem, semstate, reads=(), writes=()):
        E = self.eng[queue]
        for ev in self._deps(reads, writes):
            self._wait(E, ev)
        E["obj"].dma_start(out=out, in_=in_).then_inc(sem, 16)
        semstate["v"] = semstate.get("v", 0) + 16
        ev = ("d", sem, semstate["v"])
        self._record(ev, reads, writes)
        return ev

    def barrier(self):
        names = ["pe", "act", "dve"]
        evs = {}
        for n in names:
            E = self.eng[n]
            if E["insts"]:
                evs[n] = ("e", n, len(E["insts"]) - 1)
        for n in names + ["pool", "sp"]:
            for m, ev in evs.items():
                if m != n:
                    self._wait(self.eng[n], ev)


def tiles_of(t0, n):
    return range(t0 // 128, (t0 + n - 1) // 128 + 1)


def build_program(stage=3):
    nc = bass.Bass("TRN2", target_bir_lowering=False)
    dt_in = lambda name, shape: nc.dram_tensor(name, shape, F32, kind="ExternalInput").ap()
    xin = dt_in("xin", [T, D])
    cst_d = dt_in("cst", [128, NCOL])
    msk_d = dt_in("msk", [128, 512])
    idn_d = dt_in("idn", [128, 128])
    w1g = dt_in("w1g", [D, DFF]); w1u = dt_in("w1u", [D, DFF]); w1d = dt_in("w1d", [DFF, D])
    w2g = dt_in("w2g", [D, DFF]); w2u = dt_in("w2u", [D, DFF]); w2d = dt_in("w2d", [DFF, D])
    win = dt_in("win", [D, INW]); wout = dt_in("wout", [D, D])
    y = nc.dram_tensor("y", [TOWN, D], F32, kind="ExternalOutput").ap()

    with ExitStack() as stack:
        tr = Tracker(nc, stack)
        sb = lambda name, shape, dt, st=stack: st.enter_context(nc.sbuf_tensor(name, shape, dt))

        xb = sb("xb", [128, KC, T], BF16)
        r = sb("r", [128, KC, T], F32)
        NW = 5
        wring = sb("wring", [128, NW, KC, 128], BF16)
        cst = sb("cst_sb", [128, NCOL], F32)
        cst2 = sb("cst2", [128, NCOL2], F32)
        msk = sb("msk_sb", [128, 512], F32)
        idf = sb("idf", [128, 128], F32)
        idb = sb("idb", [128, 128], BF16)
        avgD = sb("avgD", [128, 128], F32)
        avgC = sb("avgC", [128, 128], F32)
        ln_mean = sb("ln_mean", [128, 512], F32)
        ln_rstd = sb("ln_rstd", [128, 512], F32)
        ln_nmr = sb("ln_nmr", [128, 512], F32)
        ln_sq = sb("ln_sq", [128, 2, 512], F32R)
        avgDr = sb("avgDr", [128, 128], F32R)
        avgCr = sb("avgCr", [128, 128], F32R)
        ln_t = sb("ln_t", [128, 2, 512], F32)
        psf = [stack.enter_context(nc.psum_tensor(f"psf{i}", [128, 512], F32)) for i in range(7)]
        psb = [stack.enter_context(nc.psum_tensor(f"psb{i}", [128, 1024], BF16)) for i in range(1)]
        st = dict(pf=0, pb=0, sq=0, lt=0, ring=list(range(7)))

        def psum_f():
            ring = st["ring"]
            i = ring[st["pf"] % len(ring)]
            st["pf"] += 1
            return psf[i], ("psf", i)

        def psum_b():
            return psb[0], ("psb", 0)

        csem = tr.new_sem("cld"); cstate = {}
        tr.dma("sp", cst[:], cst_d[:, :], csem, cstate, writes=["cst"])
        tr.dma("sp", msk[:], msk_d[:, :], csem, cstate, writes=["msk"])
        tr.dma("sp", idf[:], idn_d[:, :], csem, cstate, writes=["idf"])
        for k_ in ("cst", "msk", "idf"):
            tr.last_write[k_] = ("d", csem, cstate["v"])
        tr.op("dve", lambda e: e.tensor_copy(idb[:], idf[:]), reads=["idf"], writes=["idb"])
        tr.op("dve", lambda e: e.memset(avgD[:], 1.0 / D), writes=["avgD"])
        tr.op("dve", lambda e: e.memset(avgC[:], 1.0 / 1024), writes=["avgC"])
        tr.op("dve", lambda e: e.tensor_copy(avgDr[:], avgD[:]), reads=["avgD"], writes=["avgDr"])
        tr.op("dve", lambda e: e.tensor_copy(avgCr[:], avgC[:]), reads=["avgC"], writes=["avgCr"])
        tr.op("dve", lambda e: e.memset(cst2[:, D_EPS:D_EPS + 1], EPS), writes=["cst2"])
        tr.op("dve", lambda e: e.tensor_scalar(cst2[:, 0:32], cst[:, C_LN1G:C_LN1G + 32], ALPHA, None, ALU.mult),
              reads=["cst"], writes=["cst2"])
        tr.op("dve", lambda e: e.tensor_scalar(cst2[:, 32:64], cst[:, C_LN2G:C_LN2G + 32], ALPHA, None, ALU.mult),
              reads=["cst"], writes=["cst2"])
        tr.op("dve", lambda e: e.tensor_scalar(cst2[:, D_NSINK:D_NSINK + 16], cst[:, C_SINK:C_SINK + 16], -1.0, None, ALU.mult),
              reads=["cst"], writes=["cst2"])
        tr.op("dve", lambda e: e.tensor_reduce(cst2[:, D_NMSINK:D_NMSINK + 4], cst[:, C_SINK:C_SINK + 16].rearrange("p (b h) -> p b h", h=4),
                                               AX.X, ALU.max), reads=["cst"], writes=["cst2"])
        tr.op("dve", lambda e: e.tensor_scalar(cst2[:, D_NMSINK:D_NMSINK + 4], cst2[:, D_NMSINK:D_NMSINK + 4], -1.0, None, ALU.mult),
              reads=["cst2"], writes=["cst2"])

        class WStream:
            def __init__(self, items, nslots, keyname, dst_fn, depth=None):
                self.items = items
                self.n = nslots
                self.key = keyname
                self.dst_fn = dst_fn
                self.issued = 0
                self.cur = 0
                self.sems = [tr.new_sem(keyname) for _ in range(nslots)]
                self.state = [dict() for _ in range(nslots)]
                self.depth = depth or nslots

            def _issue(self, i):
                s = i % self.n
                for src, sel in self.items[i]:
                    tr.dma("pool", self.dst_fn(s, sel), src, self.sems[s], self.state[s],
                           writes=[(self.key, s)])

            def prefetch(self, upto=None):
                while self.issued < min(len(self.items), self.cur + (upto or self.n)):
                    self._issue(self.issued)
                    self.issued += 1

            def get(self):
                i = self.cur
                while self.issued <= i:
                    self._issue(self.issued)
                    self.issued += 1
                self.cur += 1
                return i % self.n

        def colchunk(W, j):
            return [(W[:, j * 128:(j + 1) * 128].rearrange("(k p) f -> p k f", p=128), None)]

        def kdup(hk):
            src = win[:, 1024 + hk * 64:1024 + (hk + 1) * 64].rearrange("(k p) f -> p k f", p=128)
            return [(src, 0), (src, 1)]

        witems = []
        for j in range(NFC):
            witems.append(colchunk(w1g, j)); witems.append(colchunk(w1u, j))
        witems.append(kdup(0)); witems.append(kdup(1))
        witems.append(colchunk(win, 9))
        for j in range(8):
            witems.append(colchunk(win, j))
        for cc in range(8):
            witems.append(colchunk(win, 10 + cc)); witems.append(colchunk(win, 18 + cc))
        for half in range(2):
            for kk in range(KC):
                witems.append([(wout[half * 1024:(half + 1) * 1024, kk * 128:(kk + 1) * 128].rearrange("(k p) f -> p k f", p=128), "h8")])
        for j in range(NFC):
            witems.append(colchunk(w2g, j)); witems.append(colchunk(w2u, j))

        def wdst(s, sel):
            if sel is None:
                return wring[:, s]
            if sel == "h8":
                return wring[:, s, 0:8]
            return wring[:, s, :, sel * 64:(sel + 1) * 64]

        ws = WStream(witems, NW, "w", wdst)

        def xkeys(name, k, t0, n):
            return [(name, k, t) for t in tiles_of(t0, n)]

        def proj_group(slot, t0, n, out_ps, out_key, src_fn=None, nk=KC):
            if src_fn is None:
                src_fn = lambda k, t0, n: (xb[:, k, t0:t0 + n], xkeys("xb", k, t0, n))
            for k in range(nk):
                ap, keys = src_fn(k, t0, n)
                tr.op("pe", lambda e: e.matmul(out_ps[:, 0:n], wring[:, slot, k, :], ap,
                                               start=(k == 0), stop=(k == nk - 1)),
                      reads=[("w", slot)] + keys, writes=[out_key], sig=(k == nk - 1))

        def layernorm(nch, src_fn, src_keys_fn, groups, avg, emit_out, hook=None, after_group=None, stat_banks=None):
            avgr = avgDr if avg is avgD else avgCr

            sbi = [0]

            def stat_bank():
                if stat_banks is None:
                    return psum_f()
                i_ = stat_banks[sbi[0] % len(stat_banks)]; sbi[0] += 1
                return psf[i_], ("psf", i_)

            def stats(t0, n):
                p1, k1 = stat_bank()
                p2, k2 = stat_bank()
                for k in range(nch):
                    sqi = st["sq"] % 2; st["sq"] += 1
                    sk = src_keys_fn(k, t0, n)
                    tr.op("act", lambda e: e.activation(ln_sq[:, sqi, 0:n], src_fn(k, t0, n), AF.Square),
                          reads=sk, writes=[("lnsq", sqi)])
                    tr.op("pe", lambda e: e.matmul(p1[:, 0:n], avg[:], src_fn(k, t0, n),
                                                   start=(k == 0), stop=(k == nch - 1)),
                          reads=sk + ["avg"], writes=[k1])
                    tr.op("pe", lambda e: e.matmul(p2[:, 0:n], avgr[:], ln_sq[:, sqi, 0:n],
                                                   start=(k == 0), stop=(k == nch - 1)),
                          reads=[("lnsq", sqi), "avg"], writes=[k2], sig=True)
                return p1, k1, p2, k2

            def finalize(t0, n, p1, k1, p2, k2):
                tr.op("dve", lambda e: e.tensor_copy(ln_mean[:, 0:n], p1[:, 0:n]), reads=[k1], writes=["lnmean"])
                tr.op("dve", lambda e: e.tensor_tensor(ln_rstd[:, 0:n], ln_mean[:, 0:n], ln_mean[:, 0:n], ALU.mult),
                      reads=["lnmean"], writes=["lnrstd"])
                tr.op("dve", lambda e: e.tensor_tensor(ln_rstd[:, 0:n], p2[:, 0:n], ln_rstd[:, 0:n], ALU.subtract),
                      reads=[k2, "lnrstd"], writes=["lnrstd"])
                tr.op("act", lambda e: e.activation(ln_rstd[:, 0:n], ln_rstd[:, 0:n], AF.Sqrt,
                                                    bias=cst2[:, D_EPS:D_EPS + 1], scale=1.0),
                      reads=["lnrstd", "cst2"], writes=["lnrstd"])
                tr.op("dve", lambda e: e.reciprocal(ln_rstd[:, 0:n], ln_rstd[:, 0:n]), reads=["lnrstd"], writes=["lnrstd"])
                tr.op("dve", lambda e: e.scalar_tensor_tensor(ln_nmr[:, 0:n], ln_mean[:, 0:n], -1.0, ln_rstd[:, 0:n],
                                                              ALU.mult, ALU.mult),
                      reads=["lnmean", "lnrstd"], writes=["lnnmr"])

            def normalize(t0, n):
                for k in range(nch):
                    ti = st["lt"] % 2; st["lt"] += 1
                    sk = src_keys_fn(k, t0, n)
                    tr.op("dve", lambda e: e.tensor_tensor(ln_t[:, ti, 0:n], src_fn(k, t0, n), ln_rstd[:, 0:n], ALU.mult),
                          reads=sk + ["lnrstd"], writes=[("lnt", ti)])
                    tr.op("dve", lambda e: e.tensor_tensor(ln_t[:, ti, 0:n], ln_t[:, ti, 0:n], ln_nmr[:, 0:n], ALU.add),
                          reads=[("lnt", ti), "lnnmr"], writes=[("lnt", ti)])
                    emit_out(k, t0, n, ln_t[:, ti, 0:n], ("lnt", ti))
                    if hook is not None:
                        hook()

            pend = stats(*groups[0])
            for gi, (t0, n) in enumerate(groups):
                nxt = stats(*groups[gi + 1]) if gi + 1 < len(groups) else None
                finalize(t0, n, *pend)
                normalize(t0, n)
                if after_group is not None:
                    after_group(t0, n)
                pend = nxt

        def rsrc(k, t0, n):
            return r[:, k, t0:t0 + n]

        def rkeys(k, t0, n):
            return xkeys("r", k, t0, n)

        def ln_main_out(gcol, bcol, agcol, abcol, final=False):
            def emit(k, t0, n, tap, tkey):
                if not final:
                    tr.op("act", lambda e: e.activation(xb[:, k, t0:t0 + n], tap, AF.Identity,
                                                        bias=cst[:, bcol + k:bcol + k + 1], scale=cst[:, gcol + k:gcol + k + 1]),
                          reads=[tkey, "cst"], writes=xkeys("xb", k, t0, n))
                    tr.op("act", lambda e: e.activation(r[:, k, t0:t0 + n], tap, AF.Identity,
                                                        bias=cst2[:, abcol + k:abcol + k + 1], scale=cst2[:, agcol + k:agcol + k + 1]),
                          reads=[tkey, "cst2"], writes=xkeys("r", k, t0, n))
                elif k % 2 == 0:
                    tr.op("act", lambda e: e.activation(r[:, k, t0:t0 + n], tap, AF.Identity,
                                                        bias=cst[:, bcol + k:bcol + k + 1], scale=cst[:, gcol + k:gcol + k + 1]),
                          reads=[tkey, "cst"], writes=xkeys("r", k, t0, n))
                else:
                    tr.op("act", lambda e: e.activation(r[:, k, t0:t0 + n], tap, AF.Identity,
                                                        bias=cst[:, bcol + k:bcol + k + 1], scale=cst[:, gcol + k:gcol + k + 1]),
                          reads=[tkey, "cst"], writes=xkeys("r", k, t0, n))
            return emit

        def ffn(Wd, groups, tagn):
            with ExitStack() as ph:
                hT = sb(f"hT{tagn}", [128, FB, T], BF16, ph)
                sil = sb(f"sil{tagn}", [128, 2, 512], F32, ph)
                dring = sb(f"dring{tagn}", [128, 2, FB, 512], BF16, ph)
                ditems = []
                for fb in range(NFB):
                    for dq in range(4):
                        src = Wd[fb * FB * 128:(fb + 1) * FB * 128, dq * 512:(dq + 1) * 512].rearrange("(c p) d -> p c d", p=128)
                        ditems.append([(src, None)])
                ds = WStream(ditems, 2, f"d{tagn}", lambda s, sel: dring[:, s])
                si = 0
                for fb in range(NFB):
                    for c in range(FB):
                        if c == 3:
                            ds.prefetch()
                        sg = ws.get(); su = ws.get()
                        for (t0, n) in groups:
                            pg, kg = psum_f(); pu, ku = psum_f()
                            proj_group(sg, t0, n, pg, kg)
                            proj_group(su, t0, n, pu, ku)
                            s_i = si % 2; si += 1
                            tr.op("act", lambda e, s_i=s_i, pg=pg: e.activation(sil[:, s_i, 0:n], pg[:, 0:n], AF.Silu),
                                  reads=[kg], writes=[("sil", s_i)])
                            tr.op("dve", lambda e, s_i=s_i, pu=pu, c=c: e.tensor_tensor(hT[:, c, t0:t0 + n], sil[:, s_i, 0:n], pu[:, 0:n], ALU.mult),
                                  reads=[("sil", s_i), ku], writes=xkeys("h", c, t0, n))
                        ws.prefetch()
                    for dq in range(4):
                        sd = ds.get()
                        for dk in range(4):
                            kk = dq * 4 + dk
                            for (t0, n) in groups:
                                pd, kd = psum_f()
                                for c in range(FB):
                                    tr.op("pe", lambda e, c=c, pd=pd: e.matmul(pd[:, 0:n], dring[:, sd, c, dk * 128:(dk + 1) * 128],
                                                                              hT[:, c, t0:t0 + n], start=(c == 0), stop=(c == FB - 1)),
                                          reads=[(f"d{tagn}", sd)] + xkeys("h", c, t0, n), writes=[kd], sig=(c == FB - 1))
                                tr.op("dve", lambda e, pd=pd, kk=kk: e.scalar_tensor_tensor(r[:, kk, t0:t0 + n], pd[:, 0:n], 0.5, r[:, kk, t0:t0 + n],
                                                                                            ALU.mult, ALU.add),
                                      reads=[kd] + xkeys("r", kk, t0, n), writes=xkeys("r", kk, t0, n))
                        ds.prefetch()
                tr.barrier()

        ws.prefetch(upto=2)
        with ExitStack() as ph:
            NXT = 4
            xt = sb("xt", [128, NXT, D], F32, ph)
            xsem = [tr.new_sem("xl") for _ in range(NXT)]
            xstate = [dict() for _ in range(NXT)]
            for i in range(int(os.environ.get('K_NT0', T // 128))):
                s = i % NXT
                tr.dma("sp", xt[:, s], xin[i * 128:(i + 1) * 128, :], xsem[s], xstate[s], writes=[("xt", s)])
                for kq in range(4):
                    pt, kt = psum_f()
                    for c in range(4):
                        k = kq * 4 + c
                        tr.op("pe", lambda e, k=k, c=c, pt=pt: e.transpose(pt[:, c * 128:(c + 1) * 128], xt[:, s, k * 128:(k + 1) * 128], idf[:]),
                              reads=[("xt", s), "idf"], writes=[kt], sig=(c == 3))
                    pv = pt[:, :].rearrange("p (c t) -> p c t", c=4)
                    tr.op("act", lambda e, pv=pv, kq=kq: e.activation(r[:, kq * 4:(kq + 1) * 4, i * 128:(i + 1) * 128], pv, AF.Identity, scale=ALPHA),
                          reads=[kt], writes=[("r", kq * 4 + c, i) for c in range(4)])
                    tr.op("dve", lambda e, pv=pv, kq=kq: e.tensor_copy(xb[:, kq * 4:(kq + 1) * 4, i * 128:(i + 1) * 128], pv),
                          reads=[kt], writes=[("xb", kq * 4 + c, i) for c in range(4)])
            tr.barrier()
        ws.prefetch()

        G3 = [(0, 384), (384, 384), (768, 384)]
        G2 = [(128, 512), (640, 512)]

        if stage >= 1:
            if not os.environ.get("K_SKIP_FFN"):
                ffn(w1d, G3, 1)
            if not os.environ.get("K_NO_LN"):
                layernorm(KC, rsrc, rkeys, G3, avgD, ln_main_out(C_LN1G, C_LN1B, D_ALN1G, D_ALN1B))

        def mixer():
            with ExitStack() as m1:
                qT = sb("qT", [128, 8, TOWN], BF16, m1)
                kd_ = sb("kdup", [128, 2, T], BF16, m1)
                vv = sb("vv", [128, 9, 128], BF16, m1)
                hall = sb("hall", [128, 8, T], BF16, m1)
                dg = sb("dgring", [128, 16, 128], BF16, m1)
                sg_t = sb("sgt", [128, 2, 384], F32, m1)
                stt = sb("stt", [128, 4, 32], F32, m1)
                acc = xb[:].rearrange("p k t -> p (k t)").bitcast(F32)[:, 0:8 * TOWN].rearrange("p (c t) -> p c t", c=8)
                pT1 = sb("pT1", [128, 8, 128], BF16, m1)
                ao2 = sb("ao", [128, 2, 1024], BF16, m1)
                s0 = sb("s_buf0", [128, 4, 256], F32, m1)
                s_buf = [s0[:], ln_t[:].rearrange("p a (h k) -> p (a h) k", h=2)]
                s_key = [["s_buf0"], [("lnt", 0), ("lnt", 1)]]
                e_buf = [ln_mean[:].bitcast(BF16).rearrange("p (h k) -> p h k", h=4),
                         ln_rstd[:].bitcast(BF16).rearrange("p (h k) -> p h k", h=4)]
                e_key = ["lnmean", "lnrstd"]
                pT_buf = [ln_nmr[:].bitcast(BF16).rearrange("p (a b) -> p a b", a=8), pT1[:]]
                pT_key = ["lnnmr", "pT1"]

                for hk in range(2):
                    s_ = ws.get()
                    for (t0, n) in G3:
                        p, kp = psum_f()
                        proj_group(s_, t0, n, p, kp)
                        tr.op("act", lambda e: e.activation(kd_[:, hk, t0:t0 + n], p[:, 0:n], AF.Identity,
                                                            bias=cst[:, C_BKD + hk:C_BKD + hk + 1], scale=1.0),
                              reads=[kp, "cst"], writes=[("kd", hk, t) for t in tiles_of(t0, n)])
                    ws.prefetch()
                s_ = ws.get()
                for i in range(9):
                    p, kp = psum_f()
                    for k in range(KC):
                        tr.op("pe", lambda e: e.matmul(p[:, 0:128], xb[:, k, i * 128:(i + 1) * 128], wring[:, s_, k, :],
                                                       start=(k == 0), stop=(k == KC - 1)),
                              reads=[("w", s_), ("xb", k, i)], writes=[kp], sig=(k == KC - 1))
                    tr.op("dve", lambda e: e.tensor_tensor(vv[:, i, :], p[:, 0:128], cst[:, C_BV:C_BV + 128], ALU.add),
                          reads=[kp, "cst"], writes=[("v", i)])
                ws.prefetch()
                for j in range(8):
                    s_ = ws.get()
                    for (t0, n) in G2:
                        p, kp = psum_f()
                        proj_group(s_, t0, n, p, kp)
                        tr.op("act", lambda e: e.activation(qT[:, j, t0 - 128:t0 - 128 + n], p[:, 0:n], AF.Identity,
                                                            bias=cst[:, C_BIN + j:C_BIN + j + 1], scale=1.0),
                              reads=[kp, "cst"], writes=[("q", j, t) for t in tiles_of(t0, n)])
                    ws.prefetch()
                sgi = 0
                GC = [(98, 286), (384, 384), (768, 384)]
                for cc in range(8):
                    sa = ws.get(); sgt = ws.get()
                    for (t0, n) in GC:
                        pa, ka = psum_f(); pg, kg = psum_f()
                        proj_group(sa, t0, n, pa, ka)
                        proj_group(sgt, t0, n, pg, kg)
                        s_i = sgi % 2; sgi += 1
                        tr.op("act", lambda e: e.activation(sg_t[:, s_i, 0:n], pg[:, 0:n], AF.Sigmoid,
                                                            bias=cst[:, C_BIN + 18 + cc:C_BIN + 19 + cc], scale=1.0),
                              reads=[kg, "cst"], writes=[("sg", s_i)])
                        tr.op("dve", lambda e: e.scalar_tensor_tensor(hall[:, cc, t0:t0 + n], pa[:, 0:n], cst[:, C_BIN + 10 + cc:C_BIN + 11 + cc],
                                                                      sg_t[:, s_i, 0:n], ALU.add, ALU.mult),
                              reads=[ka, ("sg", s_i), "cst"], writes=[("hglu", cc)])
                    tr.op("dve", lambda e: e.tensor_scalar(hall[:, cc, 98:128], hall[:, cc, 98:128], cst[:, C_FLAG:C_FLAG + 1], None, ALU.mult),
                          reads=[("hglu", cc), "cst"], writes=[("hglu", cc)])
                    ws.prefetch()

                SA, SB, PO = 4, 5, 6
                st["ring"] = [0, 1, 2, 3]
                NB = 32

                def tb(t):
                    return 1 + t // 4, t % 4

                def S1(t):
                    i, b = tb(t)
                    hk = b // 2
                    mcol = 0 if i == 1 else 256
                    for hq in range(4):
                        h = 4 * b + hq
                        j, half = h // 2, h % 2
                        bank = SA if hq % 2 == 0 else SB
                        tr.op("pe", lambda e: e.matmul(
                            psf[bank][:, (hq // 2) * 256:(hq // 2) * 256 + 256],
                            qT[half * 64:(half + 1) * 64, j, (i - 1) * 128:i * 128],
                            kd_[half * 64:(half + 1) * 64, hk, (i - 1) * 128:(i + 1) * 128], start=True, stop=True),
                            reads=[("q", j, i), ("kd", hk, i - 1), ("kd", hk, i)], writes=[("psf", bank)], sig=(hq >= 2))
                    for x, bank in enumerate((SA, SB)):
                        tr.op("dve", lambda e: e.scalar_tensor_tensor(
                            s_buf[t % 2][:, 2 * x:2 * x + 2, :], psf[bank][:, :].rearrange("p (h k) -> p h k", h=2), 0.125,
                            msk[:, mcol:mcol + 256].unsqueeze(1).to_broadcast([128, 2, 256]), ALU.mult, ALU.add),
                            reads=[("psf", bank), "msk"], writes=s_key[t % 2])

                def S2(t):
                    i, b = tb(t)
                    sv = stt[:, t % 4, :]
                    sk = ("stt", t % 4)
                    sb_ = s_buf[t % 2]
                    tr.op("dve", lambda e: e.tensor_reduce(sv[:, 0:4], sb_, AX.X, ALU.max), reads=s_key[t % 2], writes=[sk])
                    tr.op("dve", lambda e: e.scalar_tensor_tensor(sv[:, 4:8], sv[:, 0:4], -1.0, cst2[:, D_NSINK + 4 * b:D_NSINK + 4 * b + 4],
                                                                  ALU.mult, ALU.min),
                          reads=[sk, "cst2"], writes=[sk])
                    tr.op("dve", lambda e: e.tensor_tensor(sv[:, 12:16], cst[:, C_SINK + 4 * b:C_SINK + 4 * b + 4], sv[:, 4:8], ALU.add),
                          reads=[sk, "cst"], writes=[sk])
                    tr.op("dve", lambda e: e.memset(sv[:, 8:12], 0.0), writes=[sk])
                    for p_ in range(4):
                        tr.op("act", lambda e: e.activation(e_buf[t % 2][:, p_, :], sb_[:, p_, :], AF.Exp,
                                                            bias=sv[:, 4 + p_:5 + p_], scale=1.0, accum_out=sv[:, 8 + p_:9 + p_]),
                              reads=s_key[t % 2] + [sk], writes=[e_key[t % 2], sk])
                    tr.op("act", lambda e: e.activation(sv[:, 16:20], sv[:, 12:16], AF.Exp), reads=[sk], writes=[sk])

                def S3(t):
                    pb, kb = psum_b()
                    for p_ in range(4):
                        for blk in range(2):
                            tr.op("pe", lambda e: e.transpose(pb[:, (p_ * 2 + blk) * 128:(p_ * 2 + blk + 1) * 128],
                                                              e_buf[t % 2][:, p_, blk * 128:(blk + 1) * 128], idb[:]),
                                  reads=[e_key[t % 2], "idb"], writes=[kb], sig=(p_ == 3 and blk == 1))
                    tr.op("act", lambda e: e.activation(pT_buf[t % 2].rearrange("p a b -> p (a b)"), pb[:, :], AF.Copy),
                          reads=[kb], writes=[pT_key[t % 2]])

                def S4(t):
                    i, b = tb(t)
                    hk = b // 2
                    sv = stt[:, t % 4, :]
                    sk = ("stt", t % 4)
                    po = psf[PO]
                    ao = ao2[:, i % 2, :]
                    aok = ("ao", i % 2)
                    for p_ in range(4):
                        for blk in range(2):
                            tr.op("pe", lambda e: e.matmul(po[:, p_ * 64:(p_ + 1) * 64], pT_buf[t % 2][:, p_ * 2 + blk, :],
                                                           vv[:, i - 1 + blk, hk * 64:(hk + 1) * 64], start=(blk == 0), stop=(blk == 1)),
                                  reads=[pT_key[t % 2], ("v", i - 1 + blk)], writes=[("psf", PO)], sig=(p_ == 3 and blk == 1))
                    tr.op("dve", lambda e: e.tensor_tensor(sv[:, 16:20], sv[:, 16:20], sv[:, 8:12], ALU.add), reads=[sk], writes=[sk])
                    tr.op("dve", lambda e: e.reciprocal(sv[:, 16:20], sv[:, 16:20]), reads=[sk], writes=[sk])
                    tr.op("dve", lambda e: e.tensor_tensor(
                        ao[:, b * 256:(b + 1) * 256].rearrange("p (a two d) -> p two a d", two=2, d=64),
                        po[:, 0:256].rearrange("p (two a d) -> p two a d", two=2, a=2),
                        sv[:, 16:20].rearrange("p (two a) -> p two a", two=2).unsqueeze(3).to_broadcast([128, 2, 2, 64]), ALU.mult),
                        reads=[("psf", PO), sk], writes=[aok])
                    if b == 3:
                        def tail():
                            pb, kb = psum_b()
                            for c in range(8):
                                tr.op("pe", lambda e: e.transpose(pb[:, c * 128:(c + 1) * 128], ao[:, c * 128:(c + 1) * 128], idb[:]),
                                      reads=[aok, "idb"], writes=[kb], sig=(c == 7))
                            tr.op("act", lambda e: e.activation(qT[:, 0:8, (i - 1) * 128:i * 128],
                                                                pb[:, :].rearrange("p (c t) -> p c t", c=8), AF.Copy),
                                  reads=[kb], writes=[("q", c, i) for c in range(8)])
                        deferred.append(tail)

                deferred = []

                def attn_iter(it):
                    for stage_fn, t in ((S2, it + 2), (S1, it + 3), (S3, it + 1), (S4, it)):
                        if 0 <= t < NB:
                            stage_fn(t)

                SEG = 7
                NDG = 16
                taps = [(cc, jt) for cc in range(8) for jt in range(31)]
                segs = [taps[i_:i_ + SEG] for i_ in range(0, len(taps), SEG)]
                dslot = {}

                def gen_diags(seg):
                    for (cc, jt) in seg:
                        di = len(dslot) % NDG
                        dslot[(cc, jt)] = di
                        if len(dslot) % SEG in (2, 4, 6):
                            tr.op("act", lambda e: e.activation(dg[:, di, :], idb[:], AF.Identity,
                                                                scale=cst[:, C_CW + jt * 8 + cc:C_CW + jt * 8 + cc + 1]),
                                  reads=["idb", "cst"], writes=[("dg", di)])
                        else:
                            tr.op("dve", lambda e: e.tensor_scalar(dg[:, di, :], idb[:], cst[:, C_CW + jt * 8 + cc:C_CW + jt * 8 + cc + 1], None, ALU.mult),
                                  reads=["idb", "cst"], writes=[("dg", di)])

                it = -3
                pcs = None
                gen_diags(segs[0])
                for si, seg in enumerate(segs):
                    if si + 1 < len(segs):
                        gen_diags(segs[si + 1])
                    if it < NB:
                        attn_iter(it); it += 1
                    for (cc, jt) in seg:
                        if jt == 0:
                            pcs = [psum_f(), psum_f()]
                        di = dslot[(cc, jt)]
                        for gi in range(2):
                            pc, kc = pcs[gi]
                            o0 = 98 + jt + gi * 512
                            tr.op("pe", lambda e: e.matmul(pc[:, 0:512], dg[:, di, :], hall[:, cc, o0:o0 + 512],
                                                           start=(jt == 0), stop=(jt == 30)),
                                  reads=[("dg", di), ("hglu", cc)], writes=[kc], sig=(jt == 30))
                        if jt == 30:
                            for gi in range(2):
                                pc, kc = pcs[gi]
                                tr.op("act", lambda e: e.activation(acc[:, cc, gi * 512:(gi + 1) * 512], pc[:, 0:512], AF.Identity,
                                                                    bias=cst[:, C_CB + cc:C_CB + cc + 1], scale=1.0),
                                      reads=[kc, "cst"], writes=[("acc", cc)])
                    while deferred:
                        deferred.pop(0)()
                while it < NB:
                    attn_iter(it); it += 1
                    while deferred:
                        deferred.pop(0)()
                st["ring"] = list(range(7))

                def outproj_unit(kk, half, slot, t0, n):
                    p, kp = psum_f()
                    for k in range(8):
                        if half == 0:
                            ap, keys = qT[:, k, t0 - 128:t0 - 128 + n], [("q", k, t) for t in tiles_of(t0, n)]
                        else:
                            ap, keys = hall[:, k, t0 - 128:t0 - 128 + n], [("hglu", k)]
                        tr.op("pe", lambda e: e.matmul(p[:, 0:n], wring[:, slot, k, :], ap, start=(k == 0), stop=(k == 7)),
                              reads=[("w", slot)] + keys, writes=[kp], sig=(k == 7))
                    if half == 0:
                        tr.op("dve", lambda e: e.tensor_tensor(r[:, kk, t0:t0 + n], p[:, 0:n], r[:, kk, t0:t0 + n], ALU.add),
                              reads=[kp] + xkeys("r", kk, t0, n), writes=xkeys("r", kk, t0, n))
                    else:
                        tr.op("dve", lambda e: e.scalar_tensor_tensor(r[:, kk, t0:t0 + n], p[:, 0:n], cst[:, C_BOUT + kk:C_BOUT + kk + 1],
                                                                      r[:, kk, t0:t0 + n], ALU.add, ALU.add),
                              reads=[kp, "cst"] + xkeys("r", kk, t0, n), writes=xkeys("r", kk, t0, n))

                def outproj_chunk(kk, half):
                    s_ = ws.get()
                    for (t0, n) in G2:
                        outproj_unit(kk, half, s_, t0, n)
                    ws.prefetch()

                pass1 = list(range(KC))

                def hook():
                    if pass1:
                        outproj_chunk(pass1.pop(0), 0)

                def conv_out(k, t0, n, tap, tkey):
                    tr.op("act", lambda e: e.activation(hall[:, k, t0:t0 + n], tap, AF.Silu,
                                                        bias=cst[:, C_CLB + k:C_CLB + k + 1], scale=cst[:, C_CLG + k:C_CLG + k + 1]),
                          reads=[tkey, "cst"], writes=[("hglu", k)])
                st["ring"] = [0, 1, 2]
                layernorm(8, lambda k, t0, n: acc[:, k, t0:t0 + n], lambda k, t0, n: [("acc", k)],
                          [(0, 512), (512, 512)], avgC, conv_out, hook=hook, stat_banks=[3, 4, 5, 6])
                st["ring"] = list(range(7))
                while pass1:
                    hook()
                for kk in range(KC):
                    outproj_chunk(kk, 1)
                tr.barrier()
            layernorm(KC, rsrc, rkeys, G2, avgD, ln_main_out(C_LN2G, C_LN2B, D_ALN2G, D_ALN2B))

        if stage >= 2:
            mixer()

        ffn(w2d, G2, 2)
        with ExitStack() as ph:
            NOT = 4
            ot = sb("ot", [128, NOT, D], F32, ph)
            osem = [tr.new_sem("ost") for _ in range(NOT)]
            ostate = [dict() for _ in range(NOT)]

            def emit_output(t0, n):
                for i in tiles_of(t0, n):
                    s = i % NOT
                    for kq in range(4):
                        pt, kt = psum_f()
                        for c in range(4):
                            k = kq * 4 + c
                            tr.op("pe", lambda e: e.transpose(pt[:, c * 128:(c + 1) * 128], r[:, k, i * 128:(i + 1) * 128], idf[:]),
                                  reads=[("r", k, i), "idf"], writes=[kt], sig=(c == 3))
                        tr.op("act", lambda e: e.activation(ot[:, s, kq * 512:(kq + 1) * 512], pt[:, :], AF.Copy),
                              reads=[kt], writes=[("ot", s, kq)])
                    tr.dma("sp", y[(i - 1) * 128:i * 128, :], ot[:, s], osem[s], ostate[s],
                           reads=[("ot", s, kq) for kq in range(4)])

            st["ring"] = [0, 1, 2]
            layernorm(KC, rsrc, rkeys, G2, avgD, ln_main_out(C_LN3G, C_LN3B, 0, 0, final=True), after_group=emit_output,
                      stat_banks=[3, 4, 5, 6])
            for s in range(NOT):
                nc.sync.wait_ge(osem[s], ostate[s]["v"])
    return nc


_CACHE = {}


def _host_consts(inp, half):
    c = np.zeros((128, NCOL), np.float32)
    fm = lambda v: np.asarray(v, np.float32).reshape(-1, 128).T
    c[:, C_LN1G:C_LN1G + 16] = fm(inp["ln1_g"][0]); c[:, C_LN1B:C_LN1B + 16] = fm(inp["ln1_b"][0])
    c[:, C_LN2G:C_LN2G + 16] = fm(inp["ln2_g"][0]); c[:, C_LN2B:C_LN2B + 16] = fm(inp["ln2_b"][0])
    c[:, C_LN3G:C_LN3G + 16] = fm(inp["ln3_g"][0]); c[:, C_LN3B:C_LN3B + 16] = fm(inp["ln3_b"][0])
    c[:, C_BIN:C_BIN + 26] = fm(inp["b_in"][0])
    c[:, C_BOUT:C_BOUT + 16] = fm(inp["b_out"][0])
    cw = np.asarray(inp["conv_dw_w"][0], np.float32)
    for j in range(31):
        c[:, C_CW + j * 8:C_CW + j * 8 + 8] = fm(cw[j])
    c[:, C_CB:C_CB + 8] = fm(inp["conv_dw_b"][0])
    c[:, C_CLG:C_CLG + 8] = fm(inp["conv_ln_g"][0]); c[:, C_CLB:C_CLB + 8] = fm(inp["conv_ln_b"][0])
    sk = np.asarray(inp["attn_sinks"][0], np.float32).reshape(4, 4)[:, [0, 2, 1, 3]].reshape(16)
    c[:, C_SINK:C_SINK + 16] = sk[None, :]
    bin_ = np.asarray(inp["b_in"][0], np.float32)
    c[:, C_BV:C_BV + 128] = bin_[1152:1280][None, :]
    c[:, C_FLAG] = float(half)
    for hk in range(2):
        bk = bin_[1024 + hk * 64:1024 + (hk + 1) * 64]
        c[:, C_BKD + hk] = np.concatenate([bk, bk])
    return c


def _masks(half):
    qi = np.arange(128)[:, None]
    kj = np.arange(256)[None, :]
    valid = (kj >= qi + 1) & (kj <= qi + 128)
    mb = np.where(valid, 0.0, NEG).astype(np.float32)
    ma = mb.copy()
    if half == 0:
        ma[:, :128] = NEG
    return np.concatenate([ma, mb], axis=1)


def make_in_maps(inp):
    x = np.ascontiguousarray(inp["x"], np.float32)
    shared = {
        "w1g": np.ascontiguousarray(inp["ffn1_w_gate"][0], np.float32),
        "w1u": np.ascontiguousarray(inp["ffn1_w_up"][0], np.float32),
        "w1d": np.ascontiguousarray(inp["ffn1_w_down"][0], np.float32),
        "w2g": np.ascontiguousarray(inp["ffn2_w_gate"][0], np.float32),
        "w2u": np.ascontiguousarray(inp["ffn2_w_up"][0], np.float32),
        "w2d": np.ascontiguousarray(inp["ffn2_w_down"][0], np.float32),
        "win": np.ascontiguousarray(inp["w_in"][0], np.float32),
        "wout": np.ascontiguousarray(inp["w_out"][0], np.float32),
        "idn": np.eye(128, dtype=np.float32),
    }
    in_maps = []
    for c in range(8):
        b, half = c // 2, c % 2
        if half == 0:
            xi = np.concatenate([np.zeros((128, D), np.float32), x[b, 0:1024]], axis=0)
        else:
            xi = x[b, 896:2048]
        m = dict(shared)
        m["xin"] = np.ascontiguousarray(xi)
        m["cst"] = _host_consts(inp, half)
        m["msk"] = _masks(half)
        in_maps.append(m)
    return in_maps


def kernel(**inputs):
    inp = {k: np.asarray(v) for k, v in inputs.items()}
    nc = build_program()
    in_maps = make_in_maps(inp)
    res = run_bass_kernel_spmd(nc, in_maps, core_ids=list(range(8)))
    out = np.empty((4, 2048, D), np.float32)
    for c in range(8):
        b, half = c // 2, c % 2
        out[b, half * 1024:(half + 1) * 1024] = res.results[c]["y"]
    return out
```

```python
import bisect
import os
from contextlib import ExitStack

import numpy as np
import concourse.bass as bass
import concourse.mybir as mybir
from concourse.bass_utils import run_bass_kernel_spmd

F32 = mybir.dt.float32
F32R = mybir.dt.float32r
BF16 = mybir.dt.bfloat16
AF = mybir.ActivationFunctionType
ALU = mybir.AluOpType
AX = mybir.AxisListType

D = 2048
DFF = 5632
NFC = DFF // 128
KC = D // 128
T = 1152
TOWN = 1024
INW = 3328
ALPHA = 2.0 ** 0.25
EPS = 1e-5
NEG = -30000.0
FB = 11
NFB = NFC // FB

C_LN1G, C_LN1B, C_LN2G, C_LN2B, C_LN3G, C_LN3B = 0, 16, 32, 48, 64, 80
C_BIN = 96
C_BOUT = 122
C_CW = 138
C_CB = 386
C_CLG = 394
C_CLB = 402
C_SINK = 410
C_BV = 426
C_FLAG = 554
C_BKD = 555
NCOL = 557
D_ALN1G, D_ALN1B, D_ALN2G, D_ALN2B, D_EPS, D_NSINK, D_NMSINK = 0, 16, 32, 48, 64, 65, 81
NCOL2 = 85

SAME_ENGINE_SYNC = True


class Tracker:
    def __init__(self, nc, stack):
        self.nc = nc
        self.stack = stack
        self.eng = {}
        for name, obj in (("pe", nc.tensor), ("act", nc.scalar), ("dve", nc.vector),
                          ("pool", nc.gpsimd), ("sp", nc.sync)):
            sem = stack.enter_context(nc.semaphore("prog_" + name))
            self.eng[name] = dict(name=name, obj=obj, sem=sem, insts=[], sig_idx=[], sig_cnt=[],
                                  waited={})
        self.last_write = {}
        self.readers = {}
        self.nsem = 0

    def new_sem(self, name):
        self.nsem += 1
        return self.stack.enter_context(self.nc.semaphore(f"{name}_{self.nsem}"))

    def _signal_count(self, E, idx):
        pos = bisect.bisect_left(E["sig_idx"], idx)
        if pos < len(E["sig_idx"]):
            return E["sig_cnt"][pos]
        last = len(E["insts"]) - 1
        E["insts"][last].then_inc(E["sem"], 1)
        cnt = len(E["sig_idx"]) + 1
        E["sig_idx"].append(last)
        E["sig_cnt"].append(cnt)
        return cnt

    def _wait(self, E, ev):
        if ev is None:
            return
        if ev[0] == "e":
            P = self.eng[ev[1]]
            if P is E:
                if E["name"] in ("pe", "sp", "pool") or not SAME_ENGINE_SYNC:
                    return
            cnt = self._signal_count(P, ev[2])
            sem = P["sem"]
        else:
            sem, cnt = ev[1], ev[2]
        key = id(sem)
        if E["waited"].get(key, 0) >= cnt:
            return
        E["obj"].wait_ge(sem, cnt)
        E["waited"][key] = cnt

    def _deps(self, reads, writes):
        deps = []
        for k in reads:
            w = self.last_write.get(k)
            if w is not None:
                deps.append(w)
        for k in writes:
            w = self.last_write.get(k)
            if w is not None:
                deps.append(w)
            rd = self.readers.get(k)
            if rd:
                for en, v in rd.items():
                    if en == "_d":
                        deps.extend(v)
                    else:
                        deps.append(("e", en, v))
        return deps

    def _record(self, ev, reads, writes):
        for k in reads:
            rd = self.readers.setdefault(k, {})
            if ev[0] == "e":
                rd[ev[1]] = ev[2]
            else:
                rd.setdefault("_d", []).append(ev)
        for k in writes:
            self.last_write[k] = ev
            self.readers[k] = {}

    def op(self, eng, fn, reads=(), writes=(), sig=False):
        E = self.eng[eng]
        ps_r = [k for k in reads if isinstance(k, tuple) and k[0] in ("psf", "psb")]
        if ps_r:
            writes = list(writes) + ps_r
        for ev in self._deps(reads, writes):
            self._wait(E, ev)
        inst = fn(E["obj"])
        idx = len(E["insts"])
        E["insts"].append(inst)
        if sig or eng in ("act", "dve", "pool"):
            self._signal_count(E, idx)
        ev = ("e", eng, idx)
        self._record(ev, reads, writes)
        return ev

    def dma(self, queue, out, in_, sem, semstate, reads=(), writes=()):
        E = self.eng[queue]
        for ev in self._deps(reads, writes):
            self._wait(E, ev)
        E["obj"].dma_start(out=out, in_=in_).then_inc(sem, 16)
        semstate["v"] = semstate.get("v", 0) + 16
        ev = ("d", sem, semstate["v"])
        self._record(ev, reads, writes)
        return ev

    def barrier(self):
        names = ["pe", "act", "dve"]
        evs = {}
        for n in names:
            E = self.eng[n]
            if E["insts"]:
                evs[n] = ("e", n, len(E["insts"]) - 1)
        for n in names + ["pool", "sp"]:
            for m, ev in evs.items():
                if m != n:
                    self._wait(self.eng[n], ev)


def tiles_of(t0, n):
    return range(t0 // 128, (t0 + n - 1) // 128 + 1)


def build_program(stage=3):
    nc = bass.Bass("TRN2", target_bir_lowering=False)
    dt_in = lambda name, shape: nc.dram_tensor(name, shape, F32, kind="ExternalInput").ap()
    xin = dt_in("xin", [T, D])
    cst_d = dt_in("cst", [128, NCOL])
    msk_d = dt_in("msk", [128, 512])
    idn_d = dt_in("idn", [128, 128])
    w1g = dt_in("w1g", [D, DFF]); w1u = dt_in("w1u", [D, DFF]); w1d = dt_in("w1d", [DFF, D])
    w2g = dt_in("w2g", [D, DFF]); w2u = dt_in("w2u", [D, DFF]); w2d = dt_in("w2d", [DFF, D])
    win = dt_in("win", [D, INW]); wout = dt_in("wout", [D, D])
    y = nc.dram_tensor("y", [TOWN, D], F32, kind="ExternalOutput").ap()

    with ExitStack() as stack:
        tr = Tracker(nc, stack)
        sb = lambda name, shape, dt, st=stack: st.enter_context(nc.sbuf_tensor(name, shape, dt))

        xb = sb("xb", [128, KC, T], BF16)
        r = sb("r", [128, KC, T], F32)
        NW = 4
        wring = sb("wring", [128, NW, KC, 128], BF16)
        cst = sb("cst_sb", [128, NCOL], F32)
        cst2 = sb("cst2", [128, NCOL2], F32)
        msk = sb("msk_sb", [128, 512], F32)
        idf = sb("idf", [128, 128], F32)
        idb = sb("idb", [128, 128], BF16)
        avgD = sb("avgD", [128, 128], F32)
        avgC = sb("avgC", [128, 128], F32)
        ln_mean = sb("ln_mean", [128, 512], F32)
        ln_rstd = sb("ln_rstd", [128, 512], F32)
        ln_nmr = sb("ln_nmr", [128, 512], F32)
        ln_sq = sb("ln_sq", [128, 2, 512], F32R)
        avgDr = sb("avgDr", [128, 128], F32R)
        avgCr = sb("avgCr", [128, 128], F32R)
        ln_t = sb("ln_t", [128, 2, 512], F32)
        psf = [stack.enter_context(nc.psum_tensor(f"psf{i}", [128, 512], F32)) for i in range(7)]
        psb = [stack.enter_context(nc.psum_tensor(f"psb{i}", [128, 1024], BF16)) for i in range(1)]
        st = dict(pf=0, pb=0, sq=0, lt=0, ring=list(range(7)))

        def psum_f():
            ring = st["ring"]
            i = ring[st["pf"] % len(ring)]
            st["pf"] += 1
            return psf[i], ("psf", i)

        def psum_b():
            return psb[0], ("psb", 0)

        csem = tr.new_sem("cld"); cstate = {}
        tr.dma("sp", cst[:], cst_d[:, :], csem, cstate, writes=["cst"])
        tr.dma("sp", msk[:], msk_d[:, :], csem, cstate, writes=["msk"])
        tr.dma("sp", idf[:], idn_d[:, :], csem, cstate, writes=["idf"])
        for k_ in ("cst", "msk", "idf"):
            tr.last_write[k_] = ("d", csem, cstate["v"])
        tr.op("dve", lambda e: e.tensor_copy(idb[:], idf[:]), reads=["idf"], writes=["idb"])
        tr.op("dve", lambda e: e.memset(avgD[:], 1.0 / D), writes=["avgD"])
        tr.op("dve", lambda e: e.memset(avgC[:], 1.0 / 1024), writes=["avgC"])
        tr.op("dve", lambda e: e.tensor_copy(avgDr[:], avgD[:]), reads=["avgD"], writes=["avgDr"])
        tr.op("dve", lambda e: e.tensor_copy(avgCr[:], avgC[:]), reads=["avgC"], writes=["avgCr"])
        tr.op("dve", lambda e: e.memset(cst2[:, D_EPS:D_EPS + 1], EPS), writes=["cst2"])
        tr.op("dve", lambda e: e.tensor_scalar(cst2[:, 0:32], cst[:, C_LN1G:C_LN1G + 32], ALPHA, None, ALU.mult),
              reads=["cst"], writes=["cst2"])
        tr.op("dve", lambda e: e.tensor_scalar(cst2[:, 32:64], cst[:, C_LN2G:C_LN2G + 32], ALPHA, None, ALU.mult),
              reads=["cst"], writes=["cst2"])
        tr.op("dve", lambda e: e.tensor_scalar(cst2[:, D_NSINK:D_NSINK + 16], cst[:, C_SINK:C_SINK + 16], -1.0, None, ALU.mult),
              reads=["cst"], writes=["cst2"])
        tr.op("dve", lambda e: e.tensor_reduce(cst2[:, D_NMSINK:D_NMSINK + 4], cst[:, C_SINK:C_SINK + 16].rearrange("p (b h) -> p b h", h=4),
                                               AX.X, ALU.max), reads=["cst"], writes=["cst2"])
        tr.op("dve", lambda e: e.tensor_scalar(cst2[:, D_NMSINK:D_NMSINK + 4], cst2[:, D_NMSINK:D_NMSINK + 4], -1.0, None, ALU.mult),
              reads=["cst2"], writes=["cst2"])

        class WStream:
            def __init__(self, items, nslots, keyname, dst_fn, depth=None):
                self.items = items
                self.n = nslots
                self.key = keyname
                self.dst_fn = dst_fn
                self.issued = 0
                self.cur = 0
                self.sems = [tr.new_sem(keyname) for _ in range(nslots)]
                self.state = [dict() for _ in range(nslots)]
                self.depth = depth or nslots

            def _issue(self, i):
                s = i % self.n
                for src, sel in self.items[i]:
                    tr.dma("pool", self.dst_fn(s, sel), src, self.sems[s], self.state[s],
                           writes=[(self.key, s)])

            def prefetch(self, upto=None):
                while self.issued < min(len(self.items), self.cur + (upto or self.n)):
                    self._issue(self.issued)
                    self.issued += 1

            def get(self):
                i = self.cur
                while self.issued <= i:
                    self._issue(self.issued)
                    self.issued += 1
                self.cur += 1
                return i % self.n

        def colchunk(W, j):
            return [(W[:, j * 128:(j + 1) * 128].rearrange("(k p) f -> p k f", p=128), None)]

        def kdup(hk):
            src = win[:, 1024 + hk * 64:1024 + (hk + 1) * 64].rearrange("(k p) f -> p k f", p=128)
            return [(src, 0), (src, 1)]

        witems = []
        for j in range(NFC):
            witems.append(colchunk(w1g, j)); witems.append(colchunk(w1u, j))
        witems.append(kdup(0)); witems.append(kdup(1))
        witems.append(colchunk(win, 9))
        for j in range(8):
            witems.append(colchunk(win, j))
        for cc in range(8):
            witems.append(colchunk(win, 10 + cc)); witems.append(colchunk(win, 18 + cc))
        for half in range(2):
            for kk in range(KC):
                witems.append([(wout[half * 1024:(half + 1) * 1024, kk * 128:(kk + 1) * 128].rearrange("(k p) f -> p k f", p=128), "h8")])
        for j in range(NFC):
            witems.append(colchunk(w2g, j)); witems.append(colchunk(w2u, j))

        def wdst(s, sel):
            if sel is None:
                return wring[:, s]
            if sel == "h8":
                return wring[:, s, 0:8]
            return wring[:, s, :, sel * 64:(sel + 1) * 64]

        ws = WStream(witems, NW, "w", wdst)

        def xkeys(name, k, t0, n):
            return [(name, k, t) for t in tiles_of(t0, n)]

        def proj_group(slot, t0, n, out_ps, out_key, src_fn=None, nk=KC):
            if src_fn is None:
                src_fn = lambda k, t0, n: (xb[:, k, t0:t0 + n], xkeys("xb", k, t0, n))
            for k in range(nk):
                ap, keys = src_fn(k, t0, n)
                tr.op("pe", lambda e: e.matmul(out_ps[:, 0:n], wring[:, slot, k, :], ap,
                                               start=(k == 0), stop=(k == nk - 1)),
                      reads=[("w", slot)] + keys, writes=[out_key], sig=(k == nk - 1))

        def layernorm(nch, src_fn, src_keys_fn, groups, avg, emit_out, hook=None, after_group=None, stat_banks=None):
            avgr = avgDr if avg is avgD else avgCr

            sbi = [0]

            def stat_bank():
                if stat_banks is None:
                    return psum_f()
                i_ = stat_banks[sbi[0] % len(stat_banks)]; sbi[0] += 1
                return psf[i_], ("psf", i_)

            def stats(t0, n):
                p1, k1 = stat_bank()
                p2, k2 = stat_bank()
                for k in range(nch):
                    sqi = st["sq"] % 2; st["sq"] += 1
                    sk = src_keys_fn(k, t0, n)
                    tr.op("act", lambda e: e.activation(ln_sq[:, sqi, 0:n], src_fn(k, t0, n), AF.Square),
                          reads=sk, writes=[("lnsq", sqi)])
                    tr.op("pe", lambda e: e.matmul(p1[:, 0:n], avg[:], src_fn(k, t0, n),
                                                   start=(k == 0), stop=(k == nch - 1)),
                          reads=sk + ["avg"], writes=[k1])
                    tr.op("pe", lambda e: e.matmul(p2[:, 0:n], avgr[:], ln_sq[:, sqi, 0:n],
                                                   start=(k == 0), stop=(k == nch - 1)),
                          reads=[("lnsq", sqi), "avg"], writes=[k2], sig=True)
                return p1, k1, p2, k2

            def finalize(t0, n, p1, k1, p2, k2):
                tr.op("dve", lambda e: e.tensor_copy(ln_mean[:, 0:n], p1[:, 0:n]), reads=[k1], writes=["lnmean"])
                tr.op("dve", lambda e: e.tensor_tensor(ln_rstd[:, 0:n], ln_mean[:, 0:n], ln_mean[:, 0:n], ALU.mult),
                      reads=["lnmean"], writes=["lnrstd"])
                tr.op("dve", lambda e: e.tensor_tensor(ln_rstd[:, 0:n], p2[:, 0:n], ln_rstd[:, 0:n], ALU.subtract),
                      reads=[k2, "lnrstd"], writes=["lnrstd"])
                tr.op("act", lambda e: e.activation(ln_rstd[:, 0:n], ln_rstd[:, 0:n], AF.Sqrt,
                                                    bias=cst2[:, D_EPS:D_EPS + 1], scale=1.0),
                      reads=["lnrstd", "cst2"], writes=["lnrstd"])
                tr.op("dve", lambda e: e.reciprocal(ln_rstd[:, 0:n], ln_rstd[:, 0:n]), reads=["lnrstd"], writes=["lnrstd"])
                tr.op("dve", lambda e: e.scalar_tensor_tensor(ln_nmr[:, 0:n], ln_mean[:, 0:n], -1.0, ln_rstd[:, 0:n],
                                                              ALU.mult, ALU.mult),
                      reads=["lnmean", "lnrstd"], writes=["lnnmr"])

            def normalize(t0, n):
                for k in range(nch):
                    ti = st["lt"] % 2; st["lt"] += 1
                    sk = src_keys_fn(k, t0, n)
                    tr.op("dve", lambda e: e.tensor_tensor(ln_t[:, ti, 0:n], src_fn(k, t0, n), ln_rstd[:, 0:n], ALU.mult),
                          reads=sk + ["lnrstd"], writes=[("lnt", ti)])
                    tr.op("dve", lambda e: e.tensor_tensor(ln_t[:, ti, 0:n], ln_t[:, ti, 0:n], ln_nmr[:, 0:n], ALU.add),
                          reads=[("lnt", ti), "lnnmr"], writes=[("lnt", ti)])
                    emit_out(k, t0, n, ln_t[:, ti, 0:n], ("lnt", ti))
                    if hook is not None:
                        hook()

            pend = stats(*groups[0])
            for gi, (t0, n) in enumerate(groups):
                nxt = stats(*groups[gi + 1]) if gi + 1 < len(groups) else None
                finalize(t0, n, *pend)
                normalize(t0, n)
                if after_group is not None:
                    after_group(t0, n)
                pend = nxt

        def rsrc(k, t0, n):
            return r[:, k, t0:t0 + n]

        def rkeys(k, t0, n):
            return xkeys("r", k, t0, n)

        def ln_main_out(gcol, bcol, agcol, abcol, final=False):
            def emit(k, t0, n, tap, tkey):
                if not final:
                    tr.op("act", lambda e: e.activation(xb[:, k, t0:t0 + n], tap, AF.Identity,
                                                        bias=cst[:, bcol + k:bcol + k + 1], scale=cst[:, gcol + k:gcol + k + 1]),
                          reads=[tkey, "cst"], writes=xkeys("xb", k, t0, n))
                    tr.op("act", lambda e: e.activation(r[:, k, t0:t0 + n], tap, AF.Identity,
                                                        bias=cst2[:, abcol + k:abcol + k + 1], scale=cst2[:, agcol + k:agcol + k + 1]),
                          reads=[tkey, "cst2"], writes=xkeys("r", k, t0, n))
                elif k % 2 == 0:
                    tr.op("act", lambda e: e.activation(r[:, k, t0:t0 + n], tap, AF.Identity,
                                                        bias=cst[:, bcol + k:bcol + k + 1], scale=cst[:, gcol + k:gcol + k + 1]),
                          reads=[tkey, "cst"], writes=xkeys("r", k, t0, n))
                else:
                    tr.op("act", lambda e: e.activation(r[:, k, t0:t0 + n], tap, AF.Identity,
                                                        bias=cst[:, bcol + k:bcol + k + 1], scale=cst[:, gcol + k:gcol + k + 1]),
                          reads=[tkey, "cst"], writes=xkeys("r", k, t0, n))
            return emit

        def ffn(Wd, groups, tagn):
            with ExitStack() as ph:
                hT = sb(f"hT{tagn}", [128, FB, T], BF16, ph)
                sil = sb(f"sil{tagn}", [128, 2, 512], F32, ph)
                dring = sb(f"dring{tagn}", [128, 2, FB, 512], BF16, ph)
                ditems = []
                for fb in range(NFB):
                    for dq in range(4):
                        src = Wd[fb * FB * 128:(fb + 1) * FB * 128, dq * 512:(dq + 1) * 512].rearrange("(c p) d -> p c d", p=128)
                        ditems.append([(src, None)])
                ds = WStream(ditems, 2, f"d{tagn}", lambda s, sel: dring[:, s])
                si = 0
                for fb in range(NFB):
                    for c in range(FB):
                        if c == 3:
                            ds.prefetch()
                        sg = ws.get(); su = ws.get()
                        for (t0, n) in groups:
                            pg, kg = psum_f(); pu, ku = psum_f()
                            proj_group(sg, t0, n, pg, kg)
                            proj_group(su, t0, n, pu, ku)
                            s_i = si % 2; si += 1
                            tr.op("act", lambda e, s_i=s_i, pg=pg: e.activation(sil[:, s_i, 0:n], pg[:, 0:n], AF.Silu),
                                  reads=[kg], writes=[("sil", s_i)])
                            tr.op("dve", lambda e, s_i=s_i, pu=pu, c=c: e.tensor_tensor(hT[:, c, t0:t0 + n], sil[:, s_i, 0:n], pu[:, 0:n], ALU.mult),
                                  reads=[("sil", s_i), ku], writes=xkeys("h", c, t0, n))
                        ws.prefetch()
                    for dq in range(4):
                        sd = ds.get()
                        for dk in range(4):
                            kk = dq * 4 + dk
                            for (t0, n) in groups:
                                pd, kd = psum_f()
                                for c in range(FB):
                                    tr.op("pe", lambda e, c=c, pd=pd: e.matmul(pd[:, 0:n], dring[:, sd, c, dk * 128:(dk + 1) * 128],
                                                                              hT[:, c, t0:t0 + n], start=(c == 0), stop=(c == FB - 1)),
                                          reads=[(f"d{tagn}", sd)] + xkeys("h", c, t0, n), writes=[kd], sig=(c == FB - 1))
                                tr.op("dve", lambda e, pd=pd, kk=kk: e.scalar_tensor_tensor(r[:, kk, t0:t0 + n], pd[:, 0:n], 0.5, r[:, kk, t0:t0 + n],
                                                                                            ALU.mult, ALU.add),
                                      reads=[kd] + xkeys("r", kk, t0, n), writes=xkeys("r", kk, t0, n))
                        ds.prefetch()
                tr.barrier()

        ws.prefetch(upto=2)
        with ExitStack() as ph:
            NXT = 4
            xt = sb("xt", [128, NXT, D], F32, ph)
            xsem = [tr.new_sem("xl") for _ in range(NXT)]
            xstate = [dict() for _ in range(NXT)]
            for i in range(int(os.environ.get('K_NT0', T // 128))):
                s = i % NXT
                tr.dma("sp", xt[:, s], xin[i * 128:(i + 1) * 128, :], xsem[s], xstate[s], writes=[("xt", s)])
                for kq in range(4):
                    pt, kt = psum_f()
                    for c in range(4):
                        k = kq * 4 + c
                        tr.op("pe", lambda e, k=k, c=c, pt=pt: e.transpose(pt[:, c * 128:(c + 1) * 128], xt[:, s, k * 128:(k + 1) * 128], idf[:]),
                              reads=[("xt", s), "idf"], writes=[kt], sig=(c == 3))
                    pv = pt[:, :].rearrange("p (c t) -> p c t", c=4)
                    tr.op("act", lambda e, pv=pv, kq=kq: e.activation(r[:, kq * 4:(kq + 1) * 4, i * 128:(i + 1) * 128], pv, AF.Identity, scale=ALPHA),
                          reads=[kt], writes=[("r", kq * 4 + c, i) for c in range(4)])
                    tr.op("dve", lambda e, pv=pv, kq=kq: e.tensor_copy(xb[:, kq * 4:(kq + 1) * 4, i * 128:(i + 1) * 128], pv),
                          reads=[kt], writes=[("xb", kq * 4 + c, i) for c in range(4)])
            tr.barrier()
        ws.prefetch()

        G3 = [(0, 384), (384, 384), (768, 384)]
        G2 = [(128, 512), (640, 512)]

        if stage >= 1:
            if not os.environ.get("K_SKIP_FFN"):
                ffn(w1d, G3, 1)
            if not os.environ.get("K_NO_LN"):
                layernorm(KC, rsrc, rkeys, G3, avgD, ln_main_out(C_LN1G, C_LN1B, D_ALN1G, D_ALN1B))

        def mixer():
            with ExitStack() as m1:
                qT = sb("qT", [128, 8, TOWN], BF16, m1)
                kd_ = sb("kdup", [128, 2, T], BF16, m1)
                vv = sb("vv", [128, 9, 128], BF16, m1)
                hall = sb("hall", [128, 8, T], BF16, m1)
                dg = sb("dgring", [128, 16, 128], BF16, m1)
                sg_t = sb("sgt", [128, 2, 384], F32, m1)
                stt = sb("stt", [128, 4, 32], F32, m1)
                acc = xb[:].rearrange("p k t -> p (k t)").bitcast(F32)[:, 0:8 * TOWN].rearrange("p (c t) -> p c t", c=8)
                pT1 = sb("pT1", [128, 8, 128], BF16, m1)
                ao2 = sb("ao", [128, 2, 1024], BF16, m1)
                s0 = sb("s_buf0", [128, 4, 256], F32, m1)
                s_buf = [s0[:], ln_t[:].rearrange("p a (h k) -> p (a h) k", h=2)]
                s_key = [["s_buf0"], [("lnt", 0), ("lnt", 1)]]
                e_buf = [ln_mean[:].bitcast(BF16).rearrange("p (h k) -> p h k", h=4),
                         ln_rstd[:].bitcast(BF16).rearrange("p (h k) -> p h k", h=4)]
                e_key = ["lnmean", "lnrstd"]
                pT_buf = [ln_nmr[:].bitcast(BF16).rearrange("p (a b) -> p a b", a=8), pT1[:]]
                pT_key = ["lnnmr", "pT1"]

                for hk in range(2):
                    s_ = ws.get()
                    for (t0, n) in G3:
                        p, kp = psum_f()
                        proj_group(s_, t0, n, p, kp)
                        tr.op("act", lambda e: e.activation(kd_[:, hk, t0:t0 + n], p[:, 0:n], AF.Identity,
                                                            bias=cst[:, C_BKD + hk:C_BKD + hk + 1], scale=1.0),
                              reads=[kp, "cst"], writes=[("kd", hk, t) for t in tiles_of(t0, n)])
                    ws.prefetch()
                s_ = ws.get()
                for i in range(9):
                    p, kp = psum_f()
                    for k in range(KC):
                        tr.op("pe", lambda e: e.matmul(p[:, 0:128], xb[:, k, i * 128:(i + 1) * 128], wring[:, s_, k, :],
                                                       start=(k == 0), stop=(k == KC - 1)),
                              reads=[("w", s_), ("xb", k, i)], writes=[kp], sig=(k == KC - 1))
                    tr.op("dve", lambda e: e.tensor_tensor(vv[:, i, :], p[:, 0:128], cst[:, C_BV:C_BV + 128], ALU.add),
                          reads=[kp, "cst"], writes=[("v", i)])
                ws.prefetch()
                for j in range(8):
                    s_ = ws.get()
                    for (t0, n) in G2:
                        p, kp = psum_f()
                        proj_group(s_, t0, n, p, kp)
                        tr.op("act", lambda e: e.activation(qT[:, j, t0 - 128:t0 - 128 + n], p[:, 0:n], AF.Identity,
                                                            bias=cst[:, C_BIN + j:C_BIN + j + 1], scale=1.0),
                              reads=[kp, "cst"], writes=[("q", j, t) for t in tiles_of(t0, n)])
                    ws.prefetch()
                sgi = 0
                GC = [(98, 286), (384, 384), (768, 384)]
                for cc in range(8):
                    sa = ws.get(); sgt = ws.get()
                    for (t0, n) in GC:
                        pa, ka = psum_f(); pg, kg = psum_f()
                        proj_group(sa, t0, n, pa, ka)
                        proj_group(sgt, t0, n, pg, kg)
                        s_i = sgi % 2; sgi += 1
                        tr.op("act", lambda e: e.activation(sg_t[:, s_i, 0:n], pg[:, 0:n], AF.Sigmoid,
                                                            bias=cst[:, C_BIN + 18 + cc:C_BIN + 19 + cc], scale=1.0),
                              reads=[kg, "cst"], writes=[("sg", s_i)])
                        tr.op("dve", lambda e: e.scalar_tensor_tensor(hall[:, cc, t0:t0 + n], pa[:, 0:n], cst[:, C_BIN + 10 + cc:C_BIN + 11 + cc],
                                                                      sg_t[:, s_i, 0:n], ALU.add, ALU.mult),
                              reads=[ka, ("sg", s_i), "cst"], writes=[("hglu", cc)])
                    tr.op("dve", lambda e: e.tensor_scalar(hall[:, cc, 98:128], hall[:, cc, 98:128], cst[:, C_FLAG:C_FLAG + 1], None, ALU.mult),
                          reads=[("hglu", cc), "cst"], writes=[("hglu", cc)])
                    ws.prefetch()

                SA, SB, PO = 4, 5, 6
                st["ring"] = [0, 1, 2, 3]
                NB = 32

                def tb(t):
                    return 1 + t // 4, t % 4

                def S1(t):
                    i, b = tb(t)
                    hk = b // 2
                    mcol = 0 if i == 1 else 256
                    for hq in range(4):
                        h = 4 * b + hq
                        j, half = h // 2, h % 2
                        bank = SA if hq % 2 == 0 else SB
                        tr.op("pe", lambda e: e.matmul(
                            psf[bank][:, (hq // 2) * 256:(hq // 2) * 256 + 256],
                            qT[half * 64:(half + 1) * 64, j, (i - 1) * 128:i * 128],
                            kd_[half * 64:(half + 1) * 64, hk, (i - 1) * 128:(i + 1) * 128], start=True, stop=True),
                            reads=[("q", j, i), ("kd", hk, i - 1), ("kd", hk, i)], writes=[("psf", bank)], sig=(hq >= 2))
                    for x, bank in enumerate((SA, SB)):
                        tr.op("dve", lambda e: e.scalar_tensor_tensor(
                            s_buf[t % 2][:, 2 * x:2 * x + 2, :], psf[bank][:, :].rearrange("p (h k) -> p h k", h=2), 0.125,
                            msk[:, mcol:mcol + 256].unsqueeze(1).to_broadcast([128, 2, 256]), ALU.mult, ALU.add),
                            reads=[("psf", bank), "msk"], writes=s_key[t % 2])

                def S2(t):
                    i, b = tb(t)
                    sv = stt[:, t % 4, :]
                    sk = ("stt", t % 4)
                    sb_ = s_buf[t % 2]
                    tr.op("dve", lambda e: e.tensor_reduce(sv[:, 0:4], sb_, AX.X, ALU.max), reads=s_key[t % 2], writes=[sk])
                    tr.op("dve", lambda e: e.scalar_tensor_tensor(sv[:, 4:8], sv[:, 0:4], -1.0, cst2[:, D_NSINK + 4 * b:D_NSINK + 4 * b + 4],
                                                                  ALU.mult, ALU.min),
                          reads=[sk, "cst2"], writes=[sk])
                    tr.op("dve", lambda e: e.tensor_tensor(sv[:, 12:16], cst[:, C_SINK + 4 * b:C_SINK + 4 * b + 4], sv[:, 4:8], ALU.add),
                          reads=[sk, "cst"], writes=[sk])
                    for p_ in range(4):
                        tr.op("act", lambda e: e.activation(e_buf[t % 2][:, p_, :], sb_[:, p_, :], AF.Exp,
                                                            bias=sv[:, 4 + p_:5 + p_], scale=1.0, accum_out=sv[:, 8 + p_:9 + p_]),
                              reads=s_key[t % 2] + [sk], writes=[e_key[t % 2], sk])
                    tr.op("act", lambda e: e.activation(sv[:, 16:20], sv[:, 12:16], AF.Exp), reads=[sk], writes=[sk])

                def S3(t):
                    pb, kb = psum_b()
                    for p_ in range(4):
                        for blk in range(2):
                            tr.op("pe", lambda e: e.transpose(pb[:, (p_ * 2 + blk) * 128:(p_ * 2 + blk + 1) * 128],
                                                              e_buf[t % 2][:, p_, blk * 128:(blk + 1) * 128], idb[:]),
                                  reads=[e_key[t % 2], "idb"], writes=[kb], sig=(p_ == 3 and blk == 1))
                    tr.op("act", lambda e: e.activation(pT_buf[t % 2].rearrange("p a b -> p (a b)"), pb[:, :], AF.Copy),
                          reads=[kb], writes=[pT_key[t % 2]])

                def S4(t):
                    i, b = tb(t)
                    hk = b // 2
                    sv = stt[:, t % 4, :]
                    sk = ("stt", t % 4)
                    po = psf[PO]
                    ao = ao2[:, i % 2, :]
                    aok = ("ao", i % 2)
                    for p_ in range(4):
                        for blk in range(2):
                            tr.op("pe", lambda e: e.matmul(po[:, p_ * 64:(p_ + 1) * 64], pT_buf[t % 2][:, p_ * 2 + blk, :],
                                                           vv[:, i - 1 + blk, hk * 64:(hk + 1) * 64], start=(blk == 0), stop=(blk == 1)),
                                  reads=[pT_key[t % 2], ("v", i - 1 + blk)], writes=[("psf", PO)], sig=(p_ == 3 and blk == 1))
                    tr.op("dve", lambda e: e.tensor_tensor(sv[:, 16:20], sv[:, 16:20], sv[:, 8:12], ALU.add), reads=[sk], writes=[sk])
                    tr.op("dve", lambda e: e.reciprocal(sv[:, 16:20], sv[:, 16:20]), reads=[sk], writes=[sk])
                    tr.op("dve", lambda e: e.tensor_tensor(
                        ao[:, b * 256:(b + 1) * 256].rearrange("p (a two d) -> p two a d", two=2, d=64),
                        po[:, 0:256].rearrange("p (two a d) -> p two a d", two=2, a=2),
                        sv[:, 16:20].rearrange("p (two a) -> p two a", two=2).unsqueeze(3).to_broadcast([128, 2, 2, 64]), ALU.mult),
                        reads=[("psf", PO), sk], writes=[aok])
                    if b == 3:
                        def tail():
                            pb, kb = psum_b()
                            for c in range(8):
                                tr.op("pe", lambda e: e.transpose(pb[:, c * 128:(c + 1) * 128], ao[:, c * 128:(c + 1) * 128], idb[:]),
                                      reads=[aok, "idb"], writes=[kb], sig=(c == 7))
                            tr.op("act", lambda e: e.activation(qT[:, 0:8, (i - 1) * 128:i * 128],
                                                                pb[:, :].rearrange("p (c t) -> p c t", c=8), AF.Copy),
                                  reads=[kb], writes=[("q", c, i) for c in range(8)])
                        deferred.append(tail)

                deferred = []

                def attn_iter(it):
                    for stage_fn, t in ((S2, it + 2), (S1, it + 3), (S3, it + 1), (S4, it)):
                        if 0 <= t < NB:
                            stage_fn(t)

                SEG = 7
                NDG = 16
                taps = [(cc, jt) for cc in range(8) for jt in range(31)]
                segs = [taps[i_:i_ + SEG] for i_ in range(0, len(taps), SEG)]
                dslot = {}

                def gen_diags(seg):
                    for (cc, jt) in seg:
                        di = len(dslot) % NDG
                        dslot[(cc, jt)] = di
                        if len(dslot) % SEG in (2, 4, 6):
                            tr.op("act", lambda e: e.activation(dg[:, di, :], idb[:], AF.Identity,
                                                                scale=cst[:, C_CW + jt * 8 + cc:C_CW + jt * 8 + cc + 1]),
                                  reads=["idb", "cst"], writes=[("dg", di)])
                        else:
                            tr.op("dve", lambda e: e.tensor_scalar(dg[:, di, :], idb[:], cst[:, C_CW + jt * 8 + cc:C_CW + jt * 8 + cc + 1], None, ALU.mult),
                                  reads=["idb", "cst"], writes=[("dg", di)])

                it = -3
                pcs = None
                gen_diags(segs[0])
                for si, seg in enumerate(segs):
                    if si + 1 < len(segs):
                        gen_diags(segs[si + 1])
                    if it < NB:
                        attn_iter(it); it += 1
                    for (cc, jt) in seg:
                        if jt == 0:
                            pcs = [psum_f(), psum_f()]
                        di = dslot[(cc, jt)]
                        for gi in range(2):
                            pc, kc = pcs[gi]
                            o0 = 98 + jt + gi * 512
                            tr.op("pe", lambda e: e.matmul(pc[:, 0:512], dg[:, di, :], hall[:, cc, o0:o0 + 512],
                                                           start=(jt == 0), stop=(jt == 30)),
                                  reads=[("dg", di), ("hglu", cc)], writes=[kc], sig=(jt == 30))
                        if jt == 30:
                            for gi in range(2):
                                pc, kc = pcs[gi]
                                tr.op("act", lambda e: e.activation(acc[:, cc, gi * 512:(gi + 1) * 512], pc[:, 0:512], AF.Identity,
                                                                    bias=cst[:, C_CB + cc:C_CB + cc + 1], scale=1.0),
                                      reads=[kc, "cst"], writes=[("acc", cc)])
                    while deferred:
                        deferred.pop(0)()
                while it < NB:
                    attn_iter(it); it += 1
                    while deferred:
                        deferred.pop(0)()
                st["ring"] = list(range(7))

                def outproj_unit(kk, half, slot, t0, n):
                    p, kp = psum_f()
                    for k in range(8):
                        if half == 0:
                            ap, keys = qT[:, k, t0 - 128:t0 - 128 + n], [("q", k, t) for t in tiles_of(t0, n)]
                        else:
                            ap, keys = hall[:, k, t0 - 128:t0 - 128 + n], [("hglu", k)]
                        tr.op("pe", lambda e: e.matmul(p[:, 0:n], wring[:, slot, k, :], ap, start=(k == 0), stop=(k == 7)),
                              reads=[("w", slot)] + keys, writes=[kp], sig=(k == 7))
                    if half == 0:
                        tr.op("dve", lambda e: e.tensor_tensor(r[:, kk, t0:t0 + n], p[:, 0:n], r[:, kk, t0:t0 + n], ALU.add),
                              reads=[kp] + xkeys("r", kk, t0, n), writes=xkeys("r", kk, t0, n))
                    else:
                        tr.op("dve", lambda e: e.scalar_tensor_tensor(r[:, kk, t0:t0 + n], p[:, 0:n], cst[:, C_BOUT + kk:C_BOUT + kk + 1],
                                                                      r[:, kk, t0:t0 + n], ALU.add, ALU.add),
                              reads=[kp, "cst"] + xkeys("r", kk, t0, n), writes=xkeys("r", kk, t0, n))

                def outproj_chunk(kk, half):
                    s_ = ws.get()
                    for (t0, n) in G2:
                        outproj_unit(kk, half, s_, t0, n)
                    ws.prefetch()

                pass1 = list(range(KC))

                def hook():
                    if pass1:
                        outproj_chunk(pass1.pop(0), 0)

                def conv_out(k, t0, n, tap, tkey):
                    tr.op("act", lambda e: e.activation(hall[:, k, t0:t0 + n], tap, AF.Silu,
                                                        bias=cst[:, C_CLB + k:C_CLB + k + 1], scale=cst[:, C_CLG + k:C_CLG + k + 1]),
                          reads=[tkey, "cst"], writes=[("hglu", k)])
                st["ring"] = [0, 1, 2]
                layernorm(8, lambda k, t0, n: acc[:, k, t0:t0 + n], lambda k, t0, n: [("acc", k)],
                          [(0, 512), (512, 512)], avgC, conv_out, hook=hook, stat_banks=[3, 4, 5, 6])
                st["ring"] = list(range(7))
                while pass1:
                    hook()
                for kk in range(KC):
                    outproj_chunk(kk, 1)
                tr.barrier()
            layernorm(KC, rsrc, rkeys, G2, avgD, ln_main_out(C_LN2G, C_LN2B, D_ALN2G, D_ALN2B))

        if stage >= 2:
            mixer()

        ffn(w2d, G2, 2)
        with ExitStack() as ph:
            NOT = 4
            ot = sb("ot", [128, NOT, D], F32, ph)
            osem = [tr.new_sem("ost") for _ in range(NOT)]
            ostate = [dict() for _ in range(NOT)]

            def emit_output(t0, n):
                for i in tiles_of(t0, n):
                    s = i % NOT
                    for kq in range(4):
                        pt, kt = psum_f()
                        for c in range(4):
                            k = kq * 4 + c
                            tr.op("pe", lambda e: e.transpose(pt[:, c * 128:(c + 1) * 128], r[:, k, i * 128:(i + 1) * 128], idf[:]),
                                  reads=[("r", k, i), "idf"], writes=[kt], sig=(c == 3))
                        tr.op("act", lambda e: e.activation(ot[:, s, kq * 512:(kq + 1) * 512], pt[:, :], AF.Copy),
                              reads=[kt], writes=[("ot", s, kq)])
                    tr.dma("sp", y[(i - 1) * 128:i * 128, :], ot[:, s], osem[s], ostate[s],
                           reads=[("ot", s, kq) for kq in range(4)])

            st["ring"] = [0, 1, 2]
            layernorm(KC, rsrc, rkeys, G2, avgD, ln_main_out(C_LN3G, C_LN3B, 0, 0, final=True), after_group=emit_output,
                      stat_banks=[3, 4, 5, 6])
            for s in range(NOT):
                nc.sync.wait_ge(osem[s], ostate[s]["v"])
    return nc


_CACHE = {}


def _host_consts(inp, half):
    c = np.zeros((128, NCOL), np.float32)
    fm = lambda v: np.asarray(v, np.float32).reshape(-1, 128).T
    c[:, C_LN1G:C_LN1G + 16] = fm(inp["ln1_g"][0]); c[:, C_LN1B:C_LN1B + 16] = fm(inp["ln1_b"][0])
    c[:, C_LN2G:C_LN2G + 16] = fm(inp["ln2_g"][0]); c[:, C_LN2B:C_LN2B + 16] = fm(inp["ln2_b"][0])
    c[:, C_LN3G:C_LN3G + 16] = fm(inp["ln3_g"][0]); c[:, C_LN3B:C_LN3B + 16] = fm(inp["ln3_b"][0])
    c[:, C_BIN:C_BIN + 26] = fm(inp["b_in"][0])
    c[:, C_BOUT:C_BOUT + 16] = fm(inp["b_out"][0])
    cw = np.asarray(inp["conv_dw_w"][0], np.float32)
    for j in range(31):
        c[:, C_CW + j * 8:C_CW + j * 8 + 8] = fm(cw[j])
    c[:, C_CB:C_CB + 8] = fm(inp["conv_dw_b"][0])
    c[:, C_CLG:C_CLG + 8] = fm(inp["conv_ln_g"][0]); c[:, C_CLB:C_CLB + 8] = fm(inp["conv_ln_b"][0])
    sk = np.asarray(inp["attn_sinks"][0], np.float32).reshape(4, 4)[:, [0, 2, 1, 3]].reshape(16)
    c[:, C_SINK:C_SINK + 16] = sk[None, :]
    bin_ = np.asarray(inp["b_in"][0], np.float32)
    c[:, C_BV:C_BV + 128] = bin_[1152:1280][None, :]
    c[:, C_FLAG] = float(half)
    for hk in range(2):
        bk = bin_[1024 + hk * 64:1024 + (hk + 1) * 64]
        c[:, C_BKD + hk] = np.concatenate([bk, bk])
    return c


def _masks(half):
    qi = np.arange(128)[:, None]
    kj = np.arange(256)[None, :]
    valid = (kj >= qi + 1) & (kj <= qi + 128)
    mb = np.where(valid, 0.0, NEG).astype(np.float32)
    ma = mb.copy()
    if half == 0:
        ma[:, :128] = NEG
    return np.concatenate([ma, mb], axis=1)


def make_in_maps(inp):
    x = np.ascontiguousarray(inp["x"], np.float32)
    shared = {
        "w1g": np.ascontiguousarray(inp["ffn1_w_gate"][0], np.float32),
        "w1u": np.ascontiguousarray(inp["ffn1_w_up"][0], np.float32),
        "w1d": np.ascontiguousarray(inp["ffn1_w_down"][0], np.float32),
        "w2g": np.ascontiguousarray(inp["ffn2_w_gate"][0], np.float32),
        "w2u": np.ascontiguousarray(inp["ffn2_w_up"][0], np.float32),
        "w2d": np.ascontiguousarray(inp["ffn2_w_down"][0], np.float32),
        "win": np.ascontiguousarray(inp["w_in"][0], np.float32),
        "wout": np.ascontiguousarray(inp["w_out"][0], np.float32),
        "idn": np.eye(128, dtype=np.float32),
    }
    in_maps = []
    for c in range(8):
        b, half = c // 2, c % 2
        if half == 0:
            xi = np.concatenate([np.zeros((128, D), np.float32), x[b, 0:1024]], axis=0)
        else:
            xi = x[b, 896:2048]
        m = dict(shared)
        m["xin"] = np.ascontiguousarray(xi)
        m["cst"] = _host_consts(inp, half)
        m["msk"] = _masks(half)
        in_maps.append(m)
    return in_maps


def kernel(**inputs):
    inp = {k: np.asarray(v) for k, v in inputs.items()}
    nc = build_program()
    in_maps = make_in_maps(inp)
    res = run_bass_kernel_spmd(nc, in_maps, core_ids=list(range(8)))
    out = np.empty((4, 2048, D), np.float32)
    for c in range(8):
        b, half = c // 2, c % 2
        out[b, half * 1024:(half + 1) * 1024] = res.results[c]["y"]
    return out
```
